# Optimizing a Trainium2 kernel written in Bass

```python
import jax, jax.numpy as jnp
from jax import lax
import numpy as np

D_MODEL = 1024
BATCH = 8
SEQ = 8192
DEPTH = 4

CHUNK = 64
N_A_LAYERS = DEPTH // 2
N_B_LAYERS = DEPTH - N_A_LAYERS
A_HEADS = 8
A_KEY_DIM = 128
A_VAL_DIM = D_MODEL // A_HEADS
A_FORGET_DIM = A_HEADS * A_KEY_DIM
A_IN_DIM = 2 * A_FORGET_DIM + 2 * D_MODEL
B_HEADS = 16
B_HEAD_DIM = D_MODEL // B_HEADS
B_DIM = B_HEADS * B_HEAD_DIM
B_PAST_CHUNKS = 8
B_BAND = (B_PAST_CHUNKS + 1) * CHUNK
REL_CLIP = 256
N_REL = REL_CLIP + CHUNK
FFN_DIM = -(-8 * D_MODEL // (3 * 256)) * 256
NORM_EPS = 1e-6
N_MOD = 6
MASK_VALUE = -1e30
MIN_FORGET = 1e-30

kernel_name = "yoco_hgrn2_chunk_band_attention_trunk"


def rms_norm(x, gain):
    xf = x.astype(jnp.float32)
    y = xf * lax.rsqrt(jnp.mean(xf * xf, axis=-1, keepdims=True) + NORM_EPS)
    return (y * gain.astype(jnp.float32)).astype(x.dtype)


def modulate(x, shift, scale):
    return x * (1 + scale[:, None, :]) + shift[:, None, :]


def swiglu(u, w_in, w_out):
    gate, up = jnp.split(u @ w_in, 2, axis=-1)
    return (jax.nn.silu(gate) * up) @ w_out


def hgrn2_mixer(u, w_in, w_out, lb, o_gain):
    bsz, seq, _ = u.shape
    n_chunks = seq // CHUNK
    q, z_f, v, g = jnp.split(u @ w_in, [A_FORGET_DIM, 2 * A_FORGET_DIM, 2 * A_FORGET_DIM + D_MODEL], axis=-1)
    q = jax.nn.silu(q)
    zf = z_f.astype(jnp.float32)
    lbf = lb.astype(jnp.float32)
    sig = jax.nn.sigmoid(zf)
    f = lbf + (1 - lbf) * sig
    log_f = jnp.log(jnp.maximum(f, MIN_FORGET))
    k = (1 - lbf) * (1 - sig)

    def to_chunks(t, d):
        return t.astype(jnp.float32).reshape(bsz, n_chunks, CHUNK, A_HEADS, d).transpose(1, 0, 2, 3, 4)

    xs = (to_chunks(q, A_KEY_DIM), to_chunks(k, A_KEY_DIM), to_chunks(v, A_VAL_DIM), to_chunks(log_f, A_KEY_DIM))
    causal = jnp.tril(jnp.ones((CHUNK, CHUNK), dtype=bool))

    def step(state, chunk):
        qc, kc, vc, lfc = chunk
        b = jnp.cumsum(lfc, axis=1)
        diff = b[:, :, None] - b[:, None, :]
        decay = jnp.exp(jnp.where(causal[None, :, :, None, None], diff, MASK_VALUE))
        attn = jnp.einsum('bthd,btshd,bshd->bhts', qc, decay, kc)
        o = (jnp.einsum('bhts,bshv->bthv', attn, vc)
             + jnp.einsum('bthd,bhdv->bthv', qc * jnp.exp(b), state))
        b_last = b[:, -1]
        state = (jnp.exp(b_last)[..., None] * state
                 + jnp.einsum('bshd,bshv->bhdv', kc * jnp.exp(b_last[:, None] - b), vc))
        return state, o

    state0 = jnp.zeros((bsz, A_HEADS, A_KEY_DIM, A_VAL_DIM), jnp.float32)
    _, o = lax.scan(step, state0, xs)
    o = o.transpose(1, 0, 2, 3, 4).reshape(bsz, seq, A_HEADS, A_VAL_DIM)
    o = o * lax.rsqrt(jnp.mean(o * o, axis=-1, keepdims=True) + NORM_EPS) * o_gain.astype(jnp.float32)
    o = o * jax.nn.silu(g.astype(jnp.float32)).reshape(bsz, seq, A_HEADS, A_VAL_DIM)
    return o.reshape(bsz, seq, D_MODEL).astype(u.dtype) @ w_out


def shared_kv(h, c_act, gain, mod_w, mod_b, w_kv):
    bsz, seq, _ = h.shape
    shift, scale = jnp.split(c_act @ mod_w + mod_b, 2, axis=-1)
    u = modulate(rms_norm(h, gain), shift, scale)
    k, v = jnp.split(u @ w_kv, 2, axis=-1)
    pad = ((0, 0), (B_PAST_CHUNKS * CHUNK, 0), (0, 0), (0, 0))
    k = jnp.pad(k.reshape(bsz, seq, B_HEADS, B_HEAD_DIM), pad)
    v = jnp.pad(v.reshape(bsz, seq, B_HEADS, B_HEAD_DIM), pad)
    return k, v


def chunk_band_attention(u, k_pad, v_pad, w_q, w_o, rel_bias):
    bsz, seq, _ = u.shape
    n_chunks = seq // CHUNK
    q = (u @ w_q).reshape(bsz, n_chunks, CHUNK, B_HEADS, B_HEAD_DIM).transpose(1, 0, 2, 3, 4)
    q_pos = jnp.arange(CHUNK)[:, None] + B_PAST_CHUNKS * CHUNK
    k_pos = jnp.arange(B_BAND)
    rel = jnp.clip(k_pos[None, :] - q_pos, -REL_CLIP, CHUNK - 1) + REL_CLIP
    bias = rel_bias.astype(jnp.float32)[rel].transpose(2, 0, 1)
    scale = B_HEAD_DIM ** -0.5

    def one_chunk(args):
        qc, idx = args
        kb = lax.dynamic_slice_in_dim(k_pad, idx * CHUNK, B_BAND, axis=1)
        vb = lax.dynamic_slice_in_dim(v_pad, idx * CHUNK, B_BAND, axis=1)
        s = jnp.einsum('bqhd,bkhd->bhqk', qc, kb).astype(jnp.float32) * scale + bias
        valid = k_pos >= (B_PAST_CHUNKS - idx) * CHUNK
        s = jnp.where(valid, s, MASK_VALUE)
        p = jax.nn.softmax(s, axis=-1).astype(vb.dtype)
        return jnp.einsum('bhqk,bkhd->bqhd', p, vb)

    o = lax.map(one_chunk, (q, jnp.arange(n_chunks)))
    o = o.transpose(1, 0, 2, 3, 4).reshape(bsz, seq, B_DIM)
    return o @ w_o


def setup_inputs(seed: int = 0) -> dict:
    key = jax.random.key(seed)
    ks = jax.random.split(key, 24)
    f32 = jnp.float32
    nrm = lambda k, shape, s: jax.random.normal(k, shape, f32) * s
    D = D_MODEL
    return {
        "x": nrm(ks[0], (BATCH, SEQ, D), 1.0),
        "c": nrm(ks[1], (BATCH, D), 1.0),
        "mod_w": nrm(ks[2], (DEPTH, D, N_MOD * D), 0.5 * D ** -0.5),
        "mod_b": nrm(ks[3], (DEPTH, N_MOD * D), 0.02),
        "norm_mix": 1.0 + nrm(ks[4], (DEPTH, D), 0.02),
        "norm_ffn": 1.0 + nrm(ks[5], (DEPTH, D), 0.02),
        "ffn_w_in": nrm(ks[6], (DEPTH, D, 2 * FFN_DIM), D ** -0.5),
        "ffn_w_out": nrm(ks[7], (DEPTH, FFN_DIM, D), FFN_DIM ** -0.5),
        "a_w_in": nrm(ks[8], (N_A_LAYERS, D, A_IN_DIM), D ** -0.5),
        "a_w_out": nrm(ks[9], (N_A_LAYERS, D, D), D ** -0.5),
        "a_lb": nrm(ks[10], (N_A_LAYERS, A_FORGET_DIM), 1.0),
        "a_out_norm": 1.0 + nrm(ks[11], (N_A_LAYERS, A_VAL_DIM), 0.02),
        "kv_norm": 1.0 + nrm(ks[12], (D,), 0.02),
        "kv_mod_w": nrm(ks[13], (D, 2 * D), 0.5 * D ** -0.5),
        "kv_mod_b": nrm(ks[14], (2 * D,), 0.02),
        "kv_w": nrm(ks[15], (D, 2 * B_DIM), D ** -0.5),
        "b_w_q": nrm(ks[16], (N_B_LAYERS, D, B_DIM), D ** -0.5),
        "b_w_o": nrm(ks[17], (N_B_LAYERS, B_DIM, D), B_DIM ** -0.5),
        "b_rel_bias": nrm(ks[18], (N_B_LAYERS, N_REL, B_HEADS), 0.5),
        "final_norm": 1.0 + nrm(ks[19], (D,), 0.02),
    }


def reference(x, c, mod_w, mod_b, norm_mix, norm_ffn, ffn_w_in, ffn_w_out,
              a_w_in, a_w_out, a_lb, a_out_norm,
              kv_norm, kv_mod_w, kv_mod_b, kv_w,
              b_w_q, b_w_o, b_rel_bias, final_norm):
    c_act = jax.nn.silu(c)
    sm = jax.nn.softmax(a_lb.astype(jnp.float32), axis=0)
    lower_bounds = jnp.cumsum(sm, axis=0) - sm[0]
    h = x
    k_pad = None
    v_pad = None
    for layer in range(DEPTH):
        mod = c_act @ mod_w[layer] + mod_b[layer]
        sh1, sc1, g1, sh2, sc2, g2 = jnp.split(mod, N_MOD, axis=-1)
        u = modulate(rms_norm(h, norm_mix[layer]), sh1, sc1)
        if layer < N_A_LAYERS:
            mix = hgrn2_mixer(u, a_w_in[layer], a_w_out[layer], lower_bounds[layer], a_out_norm[layer])
        else:
            j = layer - N_A_LAYERS
            mix = chunk_band_attention(u, k_pad, v_pad, b_w_q[j], b_w_o[j], b_rel_bias[j])
        h = h + g1[:, None, :] * mix
        u = modulate(rms_norm(h, norm_ffn[layer]), sh2, sc2)
        h = h + g2[:, None, :] * swiglu(u, ffn_w_in[layer], ffn_w_out[layer])
        if layer == N_A_LAYERS - 1:
            k_pad, v_pad = shared_kv(h, c_act, kv_norm, kv_mod_w, kv_mod_b, kv_w)
    return rms_norm(h, final_norm)
```

```python
import numpy as np
from contextlib import ExitStack

import concourse.bass as bass
import concourse.mybir as mybir
from concourse.bass_utils import run_bass_kernel_spmd

F32 = mybir.dt.float32
BF16 = mybir.dt.bfloat16
AF = mybir.ActivationFunctionType
ALU = mybir.AluOpType

D = 1024
P = 128
KD = 8
SEQ = 8192
NB = 8
FFN = 2816
NFC = FFN // P
A_IN = 4096
EPS = 1e-6
T = 256
NCH = T // 64
NPAIR = T // 128
NEG = -30000.0


class _Op:
    __slots__ = ("eng", "fn", "deps", "dma", "need", "sig", "idx")

    def __init__(self, eng, fn, deps, dma):
        self.eng = eng
        self.fn = fn
        self.deps = deps
        self.dma = dma
        self.need = False
        self.sig = None
        self.idx = -1


class Prog:
    ENGS = ("pe", "act", "dve", "pool", "sp")
    NDS = 32
    EPOCH = 50000

    def __init__(self, nc):
        self.nc = nc
        self.streams = {e: [] for e in self.ENGS}
        self.lastw = {}
        self.readers = {}
        self.pending = {e: [] for e in self.ENGS}
        self.dmas = []
        self.dmas_since_barrier = []
        self.nops = 0
        self.limit = None

    def op(self, eng, fn, r=(), w=(), dma=False):
        deps = []
        for k in r:
            lw = self.lastw.get(k)
            if lw is not None:
                deps.append(lw)
        for k in w:
            lw = self.lastw.get(k)
            if lw is not None:
                deps.append(lw)
            deps.extend(self.readers.get(k, ()))
        if self.pending[eng]:
            deps.extend(self.pending[eng])
            self.pending[eng] = []
        o = _Op(eng, fn, deps, dma)
        self.nops += 1
        if self.limit is not None and self.nops > self.limit:
            return o
        o.idx = len(self.streams[eng])
        self.streams[eng].append(o)
        for k in r:
            self.readers.setdefault(k, []).append(o)
        for k in w:
            self.lastw[k] = o
            self.readers[k] = []
        if dma:
            self.dmas.append(o)
            self.dmas_since_barrier.append(o)
        return o

    def barrier(self):
        b = [s[-1] for s in self.streams.values() if s]
        b.extend(self.dmas_since_barrier)
        self.dmas_since_barrier = []
        for e in self.ENGS:
            self.pending[e] = list(b)

    @staticmethod
    def _counts(dep, o):
        if dep is o:
            return False
        if dep.eng == "pe" and o.eng == "pe" and not dep.dma and not o.dma:
            return False
        return True

    def emit(self):
        nc = self.nc
        for s in self.streams.values():
            for o in s:
                for d in o.deps:
                    if self._counts(d, o):
                        d.need = True
        with ExitStack() as es:
            esems = {}
            for e in self.ENGS:
                n = sum(1 for o in self.streams[e] if o.need and not o.dma)
                ne = max(1, -(-n // self.EPOCH))
                esems[e] = [es.enter_context(nc.semaphore(f"s_{e}_{i}")) for i in range(ne)]
                cnt = 0
                for o in self.streams[e]:
                    if o.need and not o.dma:
                        o.sig = (esems[e][cnt // self.EPOCH], cnt % self.EPOCH + 1)
                        cnt += 1
            by_eng = {}
            for o in self.dmas:
                by_eng.setdefault(o.eng, []).append(o)
            for en, lst in by_eng.items():
                nds = min(self.NDS, len(lst))
                dsems = [es.enter_context(nc.semaphore(f"s_dma_{en}_{i}")) for i in range(nds)]
                for i, o in enumerate(lst):
                    o.sig = (dsems[i % nds], 16 * (i // nds + 1))
                    if i >= nds:
                        o.deps.append(lst[i - nds])
            final_waits = {}
            for o in self.dmas:
                final_waits[id(o.sig[0])] = o.sig
            engmap = {"pe": "tensor", "act": "scalar", "dve": "vector", "pool": "gpsimd", "sp": "sync"}
            block = es.enter_context(nc.Block())

            def make(ename):
                def body(e):
                    waited = {}
                    for o in self.streams[ename]:
                        for d in o.deps:
                            if not self._counts(d, o):
                                continue
                            sem, val = d.sig
                            if waited.get(id(sem), 0) >= val:
                                continue
                            e.wait_ge(sem, val)
                            waited[id(sem)] = val
                        ins = o.fn(e)
                        if o.dma:
                            ins.then_inc(o.sig[0], 16)
                        elif o.need:
                            ins.then_inc(o.sig[0], 1)
                    if ename == "sp":
                        for sem, val in final_waits.values():
                            if waited.get(id(sem), 0) < val:
                                e.wait_ge(sem, val)
                return body

            for ename in self.ENGS:
                getattr(block, engmap[ename])(make(ename))


def _pk(v):
    v = np.asarray(v, np.float32)
    return np.ascontiguousarray(v.reshape(-1, P).T)


class VecLayout:
    def __init__(self):
        self.off = {}
        self.n = 0

    def add(self, name, ncols):
        self.off[name] = (self.n, ncols)
        self.n += ncols


def _vec_layout():
    vl = VecLayout()
    vl.add("c", 8)
    for l in range(4):
        vl.add(f"modb{l}", 48)
        vl.add(f"nmix{l}", 8)
        vl.add(f"nffn{l}", 8)
    vl.add("alb0", 8)
    vl.add("alb1", 8)
    vl.add("aon0", 1)
    vl.add("aon1", 1)
    vl.add("kvn", 8)
    vl.add("kvmodb", 16)
    vl.add("fn", 8)
    vl.add("ident", 128)
    vl.add("tri", 64)
    vl.add("scanmask", T)
    return vl


VL = _vec_layout()


def _build_vecs(inp, b):
    v = np.zeros((P, VL.n), np.float32)

    def put(name, arr):
        o, n = VL.off[name]
        assert arr.shape == (P, n), (name, arr.shape, n)
        v[:, o:o + n] = arr

    put("c", _pk(inp["c"][b]))
    for l in range(4):
        put(f"modb{l}", _pk(inp["mod_b"][l]))
        put(f"nmix{l}", _pk(inp["norm_mix"][l]))
        put(f"nffn{l}", _pk(inp["norm_ffn"][l]))
    put("alb0", _pk(inp["a_lb"][0]))
    put("alb1", _pk(inp["a_lb"][1]))
    put("aon0", _pk(inp["a_out_norm"][0]))
    put("aon1", _pk(inp["a_out_norm"][1]))
    put("kvn", _pk(inp["kv_norm"]))
    put("kvmodb", _pk(inp["kv_mod_b"]))
    put("fn", _pk(inp["final_norm"]))
    put("ident", np.eye(P, dtype=np.float32))
    s = np.arange(P)[:, None] % 64
    t = np.arange(64)[None, :]
    put("tri", (s <= t).astype(np.float32))
    sm = np.ones((P, T), np.float32)
    sm[:, ::64] = 0.0
    put("scanmask", sm)
    return v


def _build_bias_blocks(rel_bias):
    rb = np.asarray(rel_bias, np.float32)
    p = np.arange(P)
    half = p // 64
    ki = p % 64
    qi = np.arange(64)
    out = np.full((2, P, 16, 10, 64), NEG, np.float32)
    for d1 in range(10):
        dl = d1 - half
        valid = (dl >= 0) & (dl <= 8)
        rel = np.clip(ki[:, None] - qi[None, :] - 64 * dl[:, None], -256, 63) + 256
        for l in range(2):
            g = rb[l][rel]
            g = np.transpose(g, (0, 2, 1))
            out[l, valid, :, d1, :] = g[valid]
    return np.ascontiguousarray(out.reshape(2, P, 16 * 640))


ALL_PHASES = ("pro", "m0", "f0", "m1", "f1", "m2", "f2", "m3", "f3")


class Builder:
    def __init__(self, S=SEQ, phases=ALL_PHASES, h_in=False, h_out=False):
        self.S = S
        self.NT = S // T
        self.phases = tuple(phases)
        self.h_in = h_in
        self.h_out = h_out
        self.nc = bass.Bass("TRN2", target_bir_lowering=False)
        self.pg = Prog(self.nc)
        self._snap = None
        self._applied = None
        self.pslot = 0
        self.pbuf = []
        self.pbanks = [0, 1]

    def op(self, eng, fn, **k):
        snap = self._snap
        if snap is None:
            return self.pg.op(eng, fn, **k)

        def fn2(e, fn=fn, snap=snap):
            if self._applied is not snap:
                self.__dict__.update(snap)
                self._applied = snap
            return fn(e)
        return self.pg.op(eng, fn2, **k)

    def freeze(self):
        self._snap = {k: v for k, v in self.__dict__.items() if k not in ("_snap", "_applied", "pg", "nc", "es")}

    def dram_in(self, name, shape, dt=F32):
        return self.nc.dram_tensor(name, list(shape), dt, kind="ExternalInput").ap()

    def pipe_add(self, mm, ev, ncols=T):
        self.pbuf.append((mm, ev, ncols))
        if len(self.pbuf) == 2:
            self.pipe_flush()

    def pipe_flush(self):
        if not self.pbuf:
            return
        bank = self.pbanks[self.pslot % len(self.pbanks)]
        self.pslot += 1
        key = ("ps", bank)
        aps = []
        off = 0
        for mm, ev, n in self.pbuf:
            ap = self.psF[:, bank, off:off + n]
            off += n
            aps.append(ap)
            mm(ap, key)
        for (mm, ev, n), ap in zip(self.pbuf, aps):
            ev(ap, key)
        self.pbuf = []

    def reset_alloc(self):
        self.aoff = 0

    def _alloc_words(self, nwords):
        nwords = (nwords + 7) // 8 * 8
        o = self.aoff
        self.aoff += nwords
        assert self.aoff <= self.RW, f"region overflow {self.aoff} > {self.RW}"
        return o

    def f32(self, n):
        o = self._alloc_words(n)
        return self.R[:, o:o + n]

    def bf(self, n):
        o = self._alloc_words((n + 1) // 2)
        return self.R[:, o:o + (n + 1) // 2].bitcast(BF16)[:, 0:n]

    def build(self):
        nc = self.nc
        S = self.S
        with ExitStack() as es:
            self.es = es
            self.xT = self.dram_in("xT", [D, S])
            self.vecs_d = self.dram_in("vecs", [P, VL.n])
            self.mod_w = self.dram_in("mod_w", [4, D, 6144])
            self.kv_mod_w = self.dram_in("kv_mod_w", [D, 2048])
            self.ffn_w_in = self.dram_in("ffn_w_in", [4, D, 2 * FFN])
            self.ffn_w_out = self.dram_in("ffn_w_out", [4, FFN, D])
            self.a_w_in = self.dram_in("a_w_in", [2, D, A_IN])
            self.a_w_out = self.dram_in("a_w_out", [2, D, D])
            self.kv_w = self.dram_in("kv_w", [D, 2048])
            self.b_w_q = self.dram_in("b_w_q", [2, D, D])
            self.b_w_o = self.dram_in("b_w_o", [2, D, D])
            self.biasblk = self.dram_in("biasblk", [2, P, 16 * 640])
            self.yT = nc.dram_tensor("yT", [D, S], F32, kind="ExternalOutput").ap()
            if self.h_in:
                self.H_in = self.dram_in("H_in", [D, S])
                self.KT_in = self.dram_in("KT_in", [D, S], BF16)
                self.VV_in = self.dram_in("VV_in", [S, D], BF16)
                self.par_in = self.dram_in("par_in", [P, 512])
            if self.h_out:
                self.H = nc.dram_tensor("H", [D, S], F32, kind="ExternalOutput").ap()
                self.KTs = nc.dram_tensor("KTs", [D, S], BF16, kind="ExternalOutput").ap()
                self.VVs = nc.dram_tensor("VVs", [S, D], BF16, kind="ExternalOutput").ap()
                self.par_out = nc.dram_tensor("par_out", [P, 512], F32, kind="ExternalOutput").ap()
            else:
                self.H = nc.dram_tensor("H", [D, S], F32).ap()
                self.KTs = nc.dram_tensor("KTs", [D, S], BF16).ap()
                self.VVs = nc.dram_tensor("VVs", [S, D], BF16).ap()

            self.vecs = es.enter_context(nc.sbuf_tensor("vecs_sb", [P, VL.n], F32))
            self.par = es.enter_context(nc.sbuf_tensor("par_sb", [P, 512], F32))
            self.cb = es.enter_context(nc.sbuf_tensor("cb_sb", [P, 512], BF16))
            self.state = es.enter_context(nc.sbuf_tensor("state_sb", [P, 8, 128], F32))
            self.cb2 = es.enter_context(nc.sbuf_tensor("cb2_sb", [P, 8], F32))
            self.RW = 44800
            self.R = es.enter_context(nc.sbuf_tensor("region", [P, self.RW], F32))
            self.psF = es.enter_context(nc.psum_tensor("psF", [P, 7, 512], F32))
            self.psB = es.enter_context(nc.psum_tensor("psB", [P, 1024], BF16))

            self.ident = self.cb[:, 0:128]
            self.ones = self.cb[:, 128:256]
            self.tri = self.cb[:, 256:320]
            self.zeros32 = self.cb[:, 320:352]

            self.freeze()
            self.setup_consts()
            ph = self.phases
            if "pro" in ph:
                self.prologue()
            else:
                self.load_params()
            first_src_is_x = "pro" in ph
            src_is_x = first_src_is_x
            mix_layers = {"m0": 0, "m1": 1, "m2": 2, "m3": 3}
            ffn_layers = {"f0": 0, "f1": 1, "f2": 2, "f3": 3}
            todo = [p for p in ph if p != "pro"]
            for i, p in enumerate(todo):
                self.pg.barrier()
                if src_is_x:
                    src = self.xT
                elif i == 0 and self.h_in:
                    src = self.H_in
                else:
                    src = self.H
                last = (p == "f3")
                dst = self.yT if last else self.H
                if p in ("m0", "m1"):
                    self.hgrn_phase(mix_layers[p], src, dst)
                elif p in ("m2", "m3"):
                    self.att_phase(mix_layers[p], src, dst, first=(i == 0 and self.h_in))
                else:
                    self.ffn_phase(ffn_layers[p], src, dst, final=last)
                src_is_x = False
            if self.h_out:
                self.pg.barrier()
                self.op("sp", lambda e: e.dma_start(out=self.par_out, in_=self.par[:]), r=[("par",)], w=[("par_out",)], dma=True)
            self.pg.emit()
        return nc

    def vcol(self, name):
        o, n = VL.off[name]
        return self.vecs[:, o:o + n]

    def setup_consts(self):
        op = self.op
        op("sp", lambda e: e.dma_start(out=self.vecs[:], in_=self.vecs_d), w=[("vecs",)], dma=True)
        op("dve", lambda e: e.tensor_copy(out=self.ident, in_=self.vcol("ident")), r=[("vecs",)], w=[("cb",)])
        op("dve", lambda e: e.memset(self.ones, 1.0), w=[("cb",)])
        op("dve", lambda e: e.memset(self.zeros32, 0.0), w=[("cb",)])
        if "pro" in self.phases:
            op("dve", lambda e: e.memset(self.par[:], 0.0), w=[("par",)])
        op("dve", lambda e: e.tensor_copy(out=self.tri, in_=self.vcol("tri")), r=[("vecs",)], w=[("cb",)])
        op("dve", lambda e: e.memset(self.state[:], 0.0), w=[("S", h) for h in range(8)])
        self.scanmask = self.vcol("scanmask")
        self.trif = self.vcol("tri")
        op("dve", lambda e: e.memset(self.cb2[:, 0:1], EPS), w=[("cb2",)])
        self.epsc = self.cb2[:, 0:1]

    def pcol(self, l, name):
        o = {"a1": 0, "b1": 8, "g1": 16, "a2": 24, "b2": 32, "g2": 40}[name]
        return self.par[:, 64 * l + o: 64 * l + o + 8]

    def hcol(self, l, name):
        o = {"s_a": 0, "ns_a": 8, "b_a": 16}[name]
        return self.par[:, 272 + 32 * l + o: 272 + 32 * l + o + 8]

    def load_params(self):
        self.op("sp", lambda e: e.dma_start(out=self.par[:], in_=self.par_in), w=[("par",)], dma=True)

    def prologue(self):
        op = self.op
        self.reset_alloc()
        self.freeze()
        cact = self.f32(8)
        op("act", lambda e: e.activation(out=cact, in_=self.vcol("c"), func=AF.Silu), r=[("vecs",)], w=[("cact",)])
        NPIECE = 1024
        wb = [self.f32(8 * NPIECE).rearrange("p (k n) -> p k n", k=8) for _ in range(2)]
        modsb = self.f32(64)
        piece = 0
        jobs = [(self.mod_w[l], 6144, l) for l in range(4)] + [(self.kv_mod_w, 2048, 4)]
        for wd, ncol, l in jobs:
            nm = ncol // P
            psm = self.psF[:, 0, 0:nm]
            for pc in range(ncol // NPIECE):
                buf = wb[piece % 2]
                bk = ("modw", piece % 2)
                piece += 1
                src = wd.rearrange("(k p) n -> p k n", p=P)[:, :, pc * NPIECE:(pc + 1) * NPIECE]
                op("sp", lambda e, buf=buf, src=src: e.dma_start(out=buf, in_=src), w=[bk], dma=True)
                for mm in range(NPIECE // P):
                    m = pc * (NPIECE // P) + mm
                    for k in range(8):
                        op("pe", lambda e, buf=buf, mm=mm, k=k, m=m, psm=psm: e.matmul(
                            psm[:, m:m + 1], lhsT=buf[:, k, mm * P:(mm + 1) * P], rhs=cact[:, k:k + 1],
                            start=(k == 0), stop=(k == 7)), r=[bk, ("cact",)], w=[("ps", 0)])
            if l < 4:
                op("dve", lambda e, psm=psm, l=l: e.tensor_tensor(out=modsb[:, 0:48], in0=psm, in1=self.vcol(f"modb{l}"), op=ALU.add),
                   r=[("ps", 0), ("vecs",)], w=[("modsb",)])
                for (nm_, ncolname, so, go, sh) in (("nmix", "a1", 8, 16, 0), ("nffn", "a2", 32, 40, 24)):
                    a = self.pcol(l, ncolname)
                    bcol = self.pcol(l, "b1" if ncolname == "a1" else "b2")
                    gcol = self.pcol(l, "g1" if ncolname == "a1" else "g2")
                    op("dve", lambda e, a=a, so=so, l=l, nm_=nm_: e.scalar_tensor_tensor(
                        out=a, in0=modsb[:, so:so + 8], scalar=1.0, in1=self.vcol(f"{nm_}{l}"), op0=ALU.add, op1=ALU.mult),
                       r=[("modsb",), ("vecs",)], w=[("par",)])
                    op("dve", lambda e, bcol=bcol, sh=sh: e.tensor_copy(out=bcol, in_=modsb[:, sh:sh + 8]), r=[("modsb",)], w=[("par",)])
                    op("dve", lambda e, gcol=gcol, go=go: e.tensor_copy(out=gcol, in_=modsb[:, go:go + 8]), r=[("modsb",)], w=[("par",)])
            else:
                op("dve", lambda e, psm=psm: e.tensor_tensor(out=modsb[:, 0:16], in0=psm, in1=self.vcol("kvmodb"), op=ALU.add),
                   r=[("ps", 0), ("vecs",)], w=[("modsb",)])
                op("dve", lambda e: e.scalar_tensor_tensor(out=self.par[:, 256:264], in0=modsb[:, 8:16], scalar=1.0, in1=self.vcol("kvn"),
                                                           op0=ALU.add, op1=ALU.mult), r=[("modsb",), ("vecs",)], w=[("par",)])
                op("dve", lambda e: e.tensor_copy(out=self.par[:, 264:272], in_=modsb[:, 0:8]), r=[("modsb",)], w=[("par",)])
        a0 = self.vcol("alb0")
        a1 = self.vcol("alb1")
        t = [self.f32(8) for _ in range(8)]
        mx, e0, e1, den, sm0, sm1, cs1, lbt = t
        seq = [
            ("dve", lambda e: e.tensor_tensor(out=mx, in0=a0, in1=a1, op=ALU.max)),
            ("dve", lambda e: e.tensor_tensor(out=e0, in0=a0, in1=mx, op=ALU.subtract)),
            ("dve", lambda e: e.tensor_tensor(out=e1, in0=a1, in1=mx, op=ALU.subtract)),
            ("act", lambda e: e.activation(out=e0, in_=e0, func=AF.Exp)),
            ("act", lambda e: e.activation(out=e1, in_=e1, func=AF.Exp)),
            ("dve", lambda e: e.tensor_tensor(out=den, in0=e0, in1=e1, op=ALU.add)),
            ("dve", lambda e: e.reciprocal(out=den, in_=den)),
            ("dve", lambda e: e.tensor_tensor(out=sm0, in0=e0, in1=den, op=ALU.mult)),
            ("dve", lambda e: e.tensor_tensor(out=sm1, in0=e1, in1=den, op=ALU.mult)),
            ("dve", lambda e: e.tensor_tensor(out=cs1, in0=sm0, in1=sm1, op=ALU.add)),
        ]
        for eng, fn in seq:
            op(eng, fn, r=[("vecs",), ("lbtmp",)], w=[("lbtmp",)])
        for l in range(2):
            cs = sm0 if l == 0 else cs1
            op("dve", lambda e, cs=cs: e.tensor_tensor(out=lbt, in0=cs, in1=sm0, op=ALU.subtract), r=[("lbtmp",)], w=[("lbtmp",)])
            op("dve", lambda e, l=l: e.tensor_scalar(out=self.hcol(l, "s_a"), in0=lbt, scalar1=-0.5, scalar2=0.5, op0=ALU.mult, op1=ALU.add),
               r=[("lbtmp",)], w=[("par",)])
            op("dve", lambda e, l=l: e.tensor_scalar(out=self.hcol(l, "ns_a"), in0=lbt, scalar1=0.5, scalar2=-0.5, op0=ALU.mult, op1=ALU.add),
               r=[("lbtmp",)], w=[("par",)])
            op("dve", lambda e, l=l: e.tensor_scalar(out=self.hcol(l, "b_a"), in0=lbt, scalar1=0.5, scalar2=0.5, op0=ALU.mult, op1=ALU.add),
               r=[("lbtmp",)], w=[("par",)])

    def tile_src(self, src, ti):
        return src.rearrange("(k p) t -> p k t", p=P)[:, :, ti * T:(ti + 1) * T]

    def load_h(self, src, ti, srckey):
        par = ti % 2
        h = self.h[par]
        self.op("sp", lambda e: e.dma_start(out=h, in_=self.tile_src(src, ti)),
                r=[(srckey, ti)], w=[("h", par, k) for k in range(8)], dma=True)

    def store_h(self, dst, ti, dstkey):
        par = ti % 2
        h = self.h[par]
        self.op("sp", lambda e: e.dma_start(out=self.tile_src(dst, ti), in_=h),
                r=[("h", par, k) for k in range(8)], w=[(dstkey, ti)], dma=True)

    def rstd(self, h, hk):
        op = self.op
        u = self.u
        uk = [("u", k) for k in range(8)]
        op("act", lambda e: e.activation(out=u.rearrange("p k t -> p (k t)"), in_=h.rearrange("p k t -> p (k t)"), func=AF.Square),
           r=hk, w=uk)

        def mm(ps, pk):
            for k in range(8):
                op("pe", lambda e, k=k: e.matmul(ps, lhsT=self.ones, rhs=u[:, k, :], start=(k == 0), stop=(k == 7)),
                   r=[uk[k], ("cb",)], w=[pk])

        def ev(ps, pk):
            op("act", lambda e: e.activation(out=self.lnt, in_=ps, func=AF.Ln, scale=1.0 / D, bias=self.epsc), r=[pk, ("cb2",)], w=[("lnt",)])
            op("act", lambda e: e.activation(out=self.rs, in_=self.lnt, func=AF.Exp, scale=-0.5), r=[("lnt",)], w=[("rs",)])

        self.pipe_flush()
        self.pipe_add(mm, ev)
        self.pipe_flush()

    def affine(self, h, hk, a, b, u, uname):
        op = self.op
        for k in range(8):
            tmp = self.tmp[k % 2]
            tk = ("tmp", k % 2)
            op("dve", lambda e, k=k, tmp=tmp: e.scalar_tensor_tensor(out=tmp, in0=h[:, k, :], scalar=a[:, k:k + 1], in1=self.rs,
                                                                     op0=ALU.mult, op1=ALU.mult),
               r=[hk[k], ("rs",), ("par",)], w=[tk])
            op("act", lambda e, k=k, tmp=tmp: e.activation(out=u[:, k, :], in_=tmp, func=AF.Identity, bias=b[:, k:k + 1], scale=1.0),
               r=[tk, ("par",)], w=[(uname, k)])

    def load_w(self, dst3, src2, nk, key):
        for k in range(nk):
            self.op("pool", lambda e, k=k: e.dma_start(out=dst3[:, k, :], in_=src2[k * P:(k + 1) * P, :]), w=[(key, k)], dma=True)

    def mm_fm(self, wt, wkey, col0, u, uname):
        def mm(ps, pk):
            for k in range(8):
                self.op("pe", lambda e, k=k: e.matmul(ps, lhsT=wt[:, k, col0:col0 + P], rhs=u[:, k, :], start=(k == 0), stop=(k == 7)),
                        r=[(wkey, k), (uname, k)], w=[pk])
        return mm

    def mm_tm(self, wt, wkey, col0, ncol, u, uname, tb):
        def mm(ps, pk):
            for k in range(8):
                self.op("pe", lambda e, k=k: e.matmul(ps, lhsT=u[:, k, tb * P:(tb + 1) * P], rhs=wt[:, k, col0:col0 + ncol],
                                                      start=(k == 0), stop=(k == 7)),
                        r=[(wkey, k), (uname, k)], w=[pk])
        return mm

    def ev_act(self, out, okey, func, scale=1.0):
        def ev(ps, pk):
            self.op("act", lambda e: e.activation(out=out, in_=ps, func=func, scale=scale), r=[pk], w=[okey])
        return ev

    def ev_dve_copy(self, out, okey):
        def ev(ps, pk):
            self.op("dve", lambda e: e.tensor_copy(out=out, in_=ps), r=[pk], w=[okey])
        return ev

    def out_proj_residual(self, wt, wkey, nk, x, xname, h, hk, g):
        op = self.op
        for m in range(8):
            def mm(ps, pk, m=m):
                for k in range(nk):
                    op("pe", lambda e, k=k: e.matmul(ps, lhsT=wt[:, k, m * P:(m + 1) * P], rhs=x[:, k, :],
                                                     start=(k == 0), stop=(k == nk - 1)),
                       r=[(wkey, k), (xname, k)], w=[pk])

            def ev(ps, pk, m=m):
                op("dve", lambda e: e.scalar_tensor_tensor(out=h[:, m, :], in0=ps, scalar=g[:, m:m + 1], in1=h[:, m, :],
                                                           op0=ALU.mult, op1=ALU.add),
                   r=[pk, hk[m], ("par",)], w=[hk[m]])
            self.pipe_add(mm, ev)
        self.pipe_flush()

    def hgrn_phase(self, L, src, dst):
        op = self.op
        self.reset_alloc()
        self.pbanks = [0, 1]
        self.win = self.bf(8 * A_IN).rearrange("p (k n) -> p k n", k=8)
        self.wout = self.bf(8 * D).rearrange("p (k n) -> p k n", k=8)
        self.u = self.bf(8 * T).rearrange("p (k t) -> p k t", k=8)
        self.q = self.bf(8 * T).rearrange("p (k t) -> p k t", k=8)
        self.g = self.bf(8 * T).rearrange("p (k t) -> p k t", k=8)
        self.on = self.bf(8 * T).rearrange("p (k t) -> p k t", k=8)
        self.v = self.bf(NPAIR * D).rearrange("p (b n) -> p b n", b=NPAIR)
        self.h = [self.f32(8 * T).rearrange("p (k t) -> p k t", k=8) for _ in range(2)]
        self.th = self.f32(8 * T).rearrange("p (k t) -> p k t", k=8)
        self.lnt = self.f32(T)
        self.rs = self.f32(T)
        self.tmp = [self.f32(T) for _ in range(2)]
        R2 = range(2)
        self.Ep = [self.bf(T) for _ in R2]
        self.Em = [self.bf(T) for _ in R2]
        self.qt = [self.bf(T) for _ in R2]
        self.kt = [self.bf(T) for _ in R2]
        self.ktT = [self.bf(NCH * P) for _ in R2]
        self.at = [self.bf(NCH * 64) for _ in R2]
        self.osq = [self.bf(T) for _ in R2]
        self.Ab = [self.bf(128) for _ in range(4)]
        self.X = [self.f32(NCH * 128) for _ in R2]
        self.lf = [self.f32(T) for _ in R2]
        self.bb = [self.f32(T) for _ in R2]
        self.bm = [self.f32(T) for _ in R2]
        self.kk = [self.f32(T) for _ in R2]
        self.lo = [self.f32(T) for _ in R2]
        self.rso = [self.f32(T) for _ in R2]
        self.t2 = [self.f32(T) for _ in R2]
        self.esm = [self.f32(3 * NCH) for _ in R2]
        self.esm2 = [self.f32(4 * NCH) for _ in R2]
        self.qta = [self.bf(T) for _ in R2]
        self.kta = [self.bf(T) for _ in R2]
        self.actr = 0
        self.freeze()

        self.load_w(self.win, self.a_w_in[L], 8, "win")
        self.load_w(self.wout, self.a_w_out[L], 8, "wout")
        for r_ in R2:
            op("pool", lambda e, r_=r_: e.memset(self.ktT[r_], 0.0), w=[("ktT", r_)])
            op("pool", lambda e, r_=r_: e.memset(self.at[r_], 0.0), w=[("at", r_)])
        op("dve", lambda e: e.memset(self.state[:], 0.0), r=[("S", h) for h in range(8)], w=[("S", h) for h in range(8)])
        srckey = "xT" if src is self.xT else ("Hin" if (self.h_in and src is self.H_in) else "H")
        dstkey = "H"
        self.load_h(src, 0, srckey)
        for ti in range(self.NT):
            if ti + 1 < self.NT:
                self.load_h(src, ti + 1, srckey)
            self.hgrn_tile(L, ti)
            self.store_h(dst, ti, dstkey)

    def hgrn_tile(self, L, ti):
        op = self.op
        par = ti % 2
        h = self.h[par]
        hk = [("h", par, k) for k in range(8)]
        u = self.u
        win = self.win
        self.rstd(h, hk)
        self.affine(h, hk, self.pcol(L, "a1"), self.pcol(L, "b1"), u, "u")
        s_a = self.hcol(L, "s_a")
        ns_a = self.hcol(L, "ns_a")
        b_a = self.hcol(L, "b_a")
        ogain = self.vcol(f"aon{L}")
        for hd in range(8):
            self.pipe_add(self.mm_fm(win, "win", hd * P, u, "u"), self.ev_act(self.q[:, hd, :], ("q", hd), AF.Silu))
            self.pipe_add(self.mm_fm(win, "win", 1024 + hd * P, u, "u"), self.ev_act(self.th[:, hd, :], ("th", hd), AF.Tanh, 0.5))
        for hd in range(8):
            self.pipe_add(self.mm_fm(win, "win", 3072 + hd * P, u, "u"), self.ev_act(self.g[:, hd, :], ("g", hd), AF.Silu))
        for tb in range(NPAIR):
            for nq in range(4):
                self.pipe_add(self.mm_tm(win, "win", 2048 + nq * 256, 256, u, "u", tb),
                              self.ev_dve_copy(self.v[:, tb, nq * 256:(nq + 1) * 256], ("v", tb, nq)))
        self.pipe_flush()
        for hd in range(8):
            r = hd % 2
            lf, bb, bm, kk = self.lf[r], self.bb[r], self.bm[r], self.kk[r]
            Ep, Em, qt, kt, ktT, at = self.Ep[r], self.Em[r], self.qt[r], self.kt[r], self.ktT[r], self.at[r]
            esm = self.esm[r]
            X = self.X[r]
            emid, elast, elm = esm[:, 0:NCH], esm[:, NCH:2 * NCH], esm[:, 2 * NCH:3 * NCH]
            th = self.th[:, hd, :]
            op("act", lambda e, hd=hd, lf=lf, th=th: e.activation(out=lf, in_=th, func=AF.Ln, scale=s_a[:, hd:hd + 1], bias=b_a[:, hd:hd + 1]),
               r=[("th", hd), ("par",)], w=[("lf", r)])
            op("dve", lambda e, hd=hd, kk=kk, th=th: e.tensor_scalar(out=kk, in0=th, scalar1=ns_a[:, hd:hd + 1], scalar2=s_a[:, hd:hd + 1],
                                                                     op0=ALU.mult, op1=ALU.add),
               r=[("th", hd), ("par",)], w=[("kk", r)])
            op("dve", lambda e, lf=lf, bb=bb: e.tensor_tensor_scan(out=bb, data0=self.scanmask, data1=lf, initial=0.0, op0=ALU.mult, op1=ALU.add),
               r=[("lf", r), ("vecs",)], w=[("bb", r)])
            bb3 = bb.rearrange("p (c t) -> p c t", t=64)
            bm3 = bm.rearrange("p (c t) -> p c t", t=64)
            op("dve", lambda e, bb3=bb3, bm3=bm3: e.tensor_tensor(out=bm3, in0=bb3, in1=bb3[:, :, 31:32].to_broadcast([P, NCH, 64]), op=ALU.subtract),
               r=[("bb", r)], w=[("bm", r)])
            op("act", lambda e, Ep=Ep, bm=bm: e.activation(out=Ep, in_=bm, func=AF.Exp), r=[("bm", r)], w=[("Ep", r)])
            op("act", lambda e, Em=Em, bm=bm: e.activation(out=Em, in_=bm, func=AF.Exp, scale=-1.0), r=[("bm", r)], w=[("Em", r)])
            op("act", lambda e, emid=emid, bb3=bb3: e.activation(out=emid, in_=bb3[:, :, 31], func=AF.Exp), r=[("bb", r)], w=[("esm", r)])
            op("act", lambda e, elast=elast, bb3=bb3: e.activation(out=elast, in_=bb3[:, :, 63], func=AF.Exp), r=[("bb", r)], w=[("esm", r)])
            op("act", lambda e, elm=elm, bm3=bm3: e.activation(out=elm, in_=bm3[:, :, 63], func=AF.Exp), r=[("bm", r)], w=[("esm", r)])
            op("dve", lambda e, hd=hd, qt=qt, Ep=Ep: e.tensor_tensor(out=qt, in0=self.q[:, hd, :], in1=Ep, op=ALU.mult),
               r=[("q", hd), ("Ep", r)], w=[("qt", r)])
            op("dve", lambda e, kt=kt, kk=kk, Em=Em: e.tensor_tensor(out=kt, in0=kk, in1=Em, op=ALU.mult),
               r=[("kk", r), ("Em", r)], w=[("kt", r)])
            pst = self.psB[:, 0:T]
            for p_ in range(NPAIR):
                op("pe", lambda e, p_=p_, kt=kt: e.transpose(out=pst[:, p_ * P:(p_ + 1) * P], in_=kt[:, p_ * P:(p_ + 1) * P], identity=self.ident),
                   r=[("kt", r), ("cb",)], w=[("psB",)])
            for hf in range(2):
                op("dve", lambda e, ktT=ktT, hf=hf: e.tensor_copy(
                    out=ktT.rearrange("p (a b d) -> p a b d", b=2, d=P)[hf * 64:(hf + 1) * 64, :, hf, :],
                    in_=pst.rearrange("p (a d) -> p a d", d=P)[hf * 64:(hf + 1) * 64, :, :]), r=[("psB",)], w=[("ktT", r)])
            bmid4 = bm.rearrange("p (j t) -> p j t", t=32)[:, :, 15]
            sa, sb = self.esm2[r][:, 0:2 * NCH], self.esm2[r][:, 2 * NCH:4 * NCH]
            op("act", lambda e, sa=sa, bmid4=bmid4: e.activation(out=sa, in_=bmid4, func=AF.Exp, scale=-1.0), r=[("bm", r)], w=[("esm2", r)])
            op("act", lambda e, sb=sb, bmid4=bmid4: e.activation(out=sb, in_=bmid4, func=AF.Exp), r=[("bm", r)], w=[("esm2", r)])
            qta, kta = self.qta[r], self.kta[r]
            op("dve", lambda e, qta=qta, qt=qt, sa=sa: e.tensor_tensor(
                out=qta.rearrange("p (j t) -> p j t", t=32), in0=qt.rearrange("p (j t) -> p j t", t=32),
                in1=sa.rearrange("p (j o) -> p j o", o=1).to_broadcast([P, 2 * NCH, 32]), op=ALU.mult),
               r=[("qt", r), ("esm2", r)], w=[("qta", r)])
            op("dve", lambda e, kta=kta, kt=kt, sb=sb: e.tensor_tensor(
                out=kta.rearrange("p (j t) -> p j t", t=32), in0=kt.rearrange("p (j t) -> p j t", t=32),
                in1=sb.rearrange("p (j o) -> p j o", o=1).to_broadcast([P, 2 * NCH, 32]), op=ALU.mult),
               r=[("kt", r), ("esm2", r)], w=[("kta", r)])
            psat = self.psF[:, 2, 0:NPAIR * 64]
            for c in range(NCH):
                p_, hf = c // 2, c % 2
                rb, cb_, t0 = hf * 64, p_ * 64, c * 64
                quads = [
                    (rb, cb_, kta[:, t0:t0 + 32], qta[:, t0:t0 + 32]),
                    (rb, cb_ + 32, kt[:, t0:t0 + 32], qt[:, t0 + 32:t0 + 64]),
                    (rb + 32, cb_ + 32, kta[:, t0 + 32:t0 + 64], qta[:, t0 + 32:t0 + 64]),
                    (rb + 32, cb_, self.zeros32, qt[:, t0:t0 + 32]),
                ]
                for (r0, c0_, l_, r_) in quads:
                    op("pe", lambda e, r0=r0, c0_=c0_, l_=l_, r_=r_: e.matmul(
                        psat[r0:r0 + 32, c0_:c0_ + 32], lhsT=l_, rhs=r_, start=True, stop=True, tile_position=(0, r0)),
                       r=[("kt", r), ("qt", r), ("kta", r), ("qta", r), ("cb",)], w=[("ps", 2)])
            for hf in range(2):
                op("dve", lambda e, at=at, hf=hf: e.tensor_tensor(
                    out=at.rearrange("p (a b t) -> p a b t", b=2, t=64)[hf * 64:(hf + 1) * 64, :, hf, :],
                    in0=psat.rearrange("p (a t) -> p a t", t=64)[hf * 64:(hf + 1) * 64, :, :],
                    in1=self.trif.rearrange("p (a t) -> p a t", a=1)[hf * 64:(hf + 1) * 64, :, :].to_broadcast([64, NPAIR, 64]), op=ALU.mult),
                   r=[("ps", 2), ("vecs",)], w=[("at", r)])
            dbank = 3 + r
            vsls = []
            for c in range(NCH):
                p_, hf = c // 2, c % 2
                vsl = self.v[:, p_, hd * P:(hd + 1) * P]
                vsls.append(vsl)
                vkeys = [("v", p_, (hd * P) // 256)]
                psd = self.psF[:, dbank, c * P:(c + 1) * P]
                op("pe", lambda e, c=c, psd=psd, ktT=ktT, vsl=vsl: e.matmul(
                    psd, lhsT=ktT[:, c * P:(c + 1) * P], rhs=vsl, start=True, stop=True),
                   r=vkeys + [("ktT", r)], w=[("ps", dbank)])
            for c in range(NCH):
                psd = self.psF[:, dbank, c * P:(c + 1) * P]
                op("act", lambda e, c=c, psd=psd, X=X, elm=elm: e.activation(out=X[:, c * P:(c + 1) * P], in_=psd, func=AF.Copy, scale=elm[:, c:c + 1]),
                   r=[("ps", dbank), ("esm", r)], w=[("X", r, c)])
            obank = 5 + r
            pso = self.psF[:, obank, 0:T]
            S_hd = self.state[:, hd, :]
            for c in range(NCH):
                p_, hf = c // 2, c % 2
                ar = self.actr % 4
                self.actr += 1
                Ab = self.Ab[ar]
                vsl = vsls[c]
                vkeys = [("v", p_, (hd * P) // 256)]
                op("act", lambda e, c=c, Ab=Ab, S_hd=S_hd, emid=emid: e.activation(out=Ab, in_=S_hd, func=AF.Copy, scale=emid[:, c:c + 1]),
                   r=[("S", hd), ("esm", r)], w=[("Ab", ar)])
                op("pe", lambda e, c=c, vsl=vsl, at=at: e.matmul(
                    pso[:, c * 64:(c + 1) * 64], lhsT=vsl, rhs=at[:, c * 64:(c + 1) * 64], start=True, stop=False),
                   r=vkeys + [("at", r)], w=[("ps", obank)])
                op("pe", lambda e, c=c, Ab=Ab, qt=qt: e.matmul(
                    pso[:, c * 64:(c + 1) * 64], lhsT=Ab, rhs=qt[:, c * 64:(c + 1) * 64], start=False, stop=True),
                   r=[("Ab", ar), ("qt", r)], w=[("ps", obank)])
                op("dve", lambda e, c=c, S_hd=S_hd, X=X, elast=elast: e.scalar_tensor_tensor(
                    out=S_hd, in0=S_hd, scalar=elast[:, c:c + 1], in1=X[:, c * P:(c + 1) * P], op0=ALU.mult, op1=ALU.add),
                   r=[("X", r, c), ("S", hd), ("esm", r)], w=[("S", hd)])
            osq, lo, rso, t2 = self.osq[r], self.lo[r], self.rso[r], self.t2[r]
            op("act", lambda e, osq=osq: e.activation(out=osq, in_=pso, func=AF.Square), r=[("ps", obank)], w=[("osq", r)])

            def mm(ps, pk, osq=osq):
                op("pe", lambda e: e.matmul(ps, lhsT=self.ones, rhs=osq, start=True, stop=True), r=[("osq", r), ("cb",)], w=[pk])

            def ev(ps, pk, lo=lo, rso=rso, r=r):
                op("act", lambda e: e.activation(out=lo, in_=ps, func=AF.Ln, scale=1.0 / 128.0, bias=self.epsc), r=[pk, ("cb2",)], w=[("lo", r)])
                op("act", lambda e: e.activation(out=rso, in_=lo, func=AF.Exp, scale=-0.5), r=[("lo", r)], w=[("rso", r)])
            self.pipe_add(mm, ev)
            self.pipe_flush()
            op("dve", lambda e, t2=t2, rso=rso: e.scalar_tensor_tensor(out=t2, in0=pso, scalar=ogain[:, 0:1], in1=rso, op0=ALU.mult, op1=ALU.mult),
               r=[("ps", obank), ("rso", r), ("vecs",)], w=[("t2", r)])
            op("pool", lambda e, hd=hd, t2=t2: e.tensor_tensor(out=self.on[:, hd, :], in0=t2, in1=self.g[:, hd, :], op=ALU.mult),
               r=[("t2", r), ("g", hd)], w=[("on", hd)])
        self.out_proj_residual(self.wout, "wout", 8, self.on, "on", h, hk, self.pcol(L, "g1"))

    def ffn_phase(self, L, src, dst, final):
        op = self.op
        self.reset_alloc()
        self.pbanks = [0, 1, 2, 3, 4, 5, 6]
        self.fwin = self.bf(8 * 2 * FFN).rearrange("p (k n) -> p k n", k=8)
        self.fwout = self.bf(NFC * D).rearrange("p (k n) -> p k n", k=NFC)
        self.u = self.bf(8 * T).rearrange("p (k t) -> p k t", k=8)
        self.actb = self.bf(NFC * T).rearrange("p (k t) -> p k t", k=NFC)
        self.sg = [self.bf(T) for _ in range(4)]
        self.h = [self.f32(8 * T).rearrange("p (k t) -> p k t", k=8) for _ in range(2)]
        self.lnt = self.f32(T)
        self.rs = self.f32(T)
        self.tmp = [self.f32(T) for _ in range(2)]
        self.freeze()
        self.load_w(self.fwin, self.ffn_w_in[L], 8, "fwin")
        self.load_w(self.fwout, self.ffn_w_out[L], NFC, "fwout")
        srckey = "Hin" if (self.h_in and src is self.H_in) else "H"
        dstkey = "yT" if final else "H"
        self.load_h(src, 0, srckey)
        for ti in range(self.NT):
            if ti + 1 < self.NT:
                self.load_h(src, ti + 1, srckey)
            self.ffn_tile(L, ti, final)
            self.store_h(dst, ti, dstkey)

    def ffn_tile(self, L, ti, final):
        op = self.op
        par = ti % 2
        h = self.h[par]
        hk = [("h", par, k) for k in range(8)]
        u = self.u
        self.rstd(h, hk)
        self.affine(h, hk, self.pcol(L, "a2"), self.pcol(L, "b2"), u, "u")
        for m in range(NFC):
            sg = self.sg[m % 4]
            sk = ("sg", m % 4)
            def ev_g(ps, pk, sg=sg, sk=sk):
                op("act", lambda e: e.activation(out=sg, in_=ps, func=AF.Silu), r=[pk], w=[sk])

            def ev_u(ps, pk, sg=sg, sk=sk, m=m):
                op("dve", lambda e: e.tensor_tensor(out=self.actb[:, m, :], in0=sg, in1=ps, op=ALU.mult), r=[sk, pk], w=[("actb", m)])
            self.pipe_flush()
            self.pipe_add(self.mm_fm(self.fwin, "fwin", m * P, u, "u"), ev_g)
            self.pipe_add(self.mm_fm(self.fwin, "fwin", FFN + m * P, u, "u"), ev_u)
        self.out_proj_residual(self.fwout, "fwout", NFC, self.actb, "actb", h, hk, self.pcol(L, "g2"))
        if final:
            self.rstd(h, hk)
            fn = self.vcol("fn")
            for k in range(8):
                op("dve", lambda e, k=k: e.scalar_tensor_tensor(out=h[:, k, :], in0=h[:, k, :], scalar=fn[:, k:k + 1], in1=self.rs,
                                                                op0=ALU.mult, op1=ALU.mult),
                   r=[hk[k], ("rs",), ("vecs",)], w=[hk[k]])

    def att_phase(self, L, src, dst, first):
        op = self.op
        j = L - 2
        self.reset_alloc()
        self.pbanks = [0, 1]
        self.wq = self.bf(8 * D).rearrange("p (k n) -> p k n", k=8)
        self.wo = self.bf(8 * D).rearrange("p (k n) -> p k n", k=8)
        if L == 2:
            self.kvw = self.bf(8 * 2 * D).rearrange("p (k n) -> p k n", k=8)
            self.ukv = self.bf(8 * T).rearrange("p (k t) -> p k t", k=8)
        self.u = self.bf(8 * T).rearrange("p (k t) -> p k t", k=8)
        self.QT = self.bf(8 * T).rearrange("p (k t) -> p k t", k=8)
        self.on = self.bf(8 * T).rearrange("p (k t) -> p k t", k=8)
        self.NR = 3 if L == 2 else 4
        self.KT = [self.bf(8 * T).rearrange("p (k t) -> p k t", k=8) for _ in range(self.NR)]
        self.VV = [self.bf(NPAIR * D).rearrange("p (b n) -> p b n", b=NPAIR) for _ in range(self.NR)]
        self.PT = [self.bf(T) for _ in range(4)]
        self.h = [self.f32(8 * T).rearrange("p (k t) -> p k t", k=8) for _ in range(2)]
        self.bias = self.f32(16 * 640).rearrange("p (h n) -> p h n", h=16)
        self.lnt = self.f32(T)
        self.rs = self.f32(T)
        self.tmp = [self.f32(T) for _ in range(2)]
        self.sc = [self.f32(T) for _ in range(4)]
        self.rden = [self.f32(T) for _ in range(2)]
        self.ptr = 0
        self.kv_from_in = (L == 3 and first)
        self.freeze()
        self.load_w(self.wq, self.b_w_q[j], 8, "wq")
        if L == 2:
            self.load_w(self.kvw, self.kv_w, 8, "kvw")
        self.load_w(self.wo, self.b_w_o[j], 8, "wo")
        op("sp", lambda e: e.dma_start(out=self.bias.rearrange("p h n -> p (h n)"), in_=self.biasblk[j]), w=[("bias",)], dma=True)
        srckey = "Hin" if (self.h_in and src is self.H_in) else "H"
        self.kv_from_in = (L == 3 and first)
        self.load_h(src, 0, srckey)
        if L == 3:
            self.load_kv(0)
        for ti in range(self.NT):
            if ti + 1 < self.NT:
                self.load_h(src, ti + 1, srckey)
                if L == 3:
                    self.load_kv(ti + 1)
            self.att_tile(L, ti)
            self.store_h(dst, ti, "H")

    def kt_dram(self, base, ti):
        return base.rearrange("(k p) t -> p k t", p=P)[:, :, ti * T:(ti + 1) * T]

    def vv_dram(self, base, ti):
        return base[ti * T:(ti + 1) * T, :].rearrange("(b p) n -> p b n", p=P)

    def load_kv(self, ti):
        rg = ti % self.NR
        kt_src = self.KT_in if self.kv_from_in else self.KTs
        vv_src = self.VV_in if self.kv_from_in else self.VVs
        self.op("sp", lambda e: e.dma_start(out=self.KT[rg], in_=self.kt_dram(kt_src, ti)),
                r=[("KTs", ti)], w=[("KT", rg, m) for m in range(8)], dma=True)
        self.op("sp", lambda e: e.dma_start(out=self.VV[rg], in_=self.vv_dram(vv_src, ti)),
                r=[("VVs", ti)], w=[("VV", rg, b, nq) for b in range(NPAIR) for nq in range(4)], dma=True)

    def att_tile(self, L, ti):
        op = self.op
        par = ti % 2
        h = self.h[par]
        hk = [("h", par, k) for k in range(8)]
        u = self.u
        rg = ti % self.NR
        self.rstd(h, hk)
        if L == 2:
            self.affine(h, hk, self.par[:, 256:264], self.par[:, 264:272], self.ukv, "ukv")
        self.affine(h, hk, self.pcol(L, "a1"), self.pcol(L, "b1"), u, "u")
        if L == 2:
            KT, VV = self.KT[rg], self.VV[rg]
            for m in range(8):
                self.pipe_add(self.mm_fm(self.kvw, "kvw", m * P, self.ukv, "ukv"), self.ev_act(KT[:, m, :], ("KT", rg, m), AF.Copy))
            for tb in range(NPAIR):
                for nq in range(4):
                    self.pipe_add(self.mm_tm(self.kvw, "kvw", D + nq * 256, 256, self.ukv, "ukv", tb),
                                  self.ev_dve_copy(VV[:, tb, nq * 256:(nq + 1) * 256], ("VV", rg, tb, nq)))
            self.pipe_flush()
            op("sp", lambda e, KT=KT: e.dma_start(out=self.kt_dram(self.KTs, ti), in_=KT),
               r=[("KT", rg, m) for m in range(8)], w=[("KTs", ti)], dma=True)
            op("sp", lambda e, VV=VV: e.dma_start(out=self.vv_dram(self.VVs, ti), in_=VV),
               r=[("VV", rg, b, nq) for b in range(NPAIR) for nq in range(4)], w=[("VVs", ti)], dma=True)
        for m in range(8):
            self.pipe_add(self.mm_fm(self.wq, "wq", m * P, u, "u"), self.ev_act(self.QT[:, m, :], ("QT", m), AF.Copy))
        self.pipe_flush()
        blocks = []
        for jb in range(6):
            tj = ti - 2 + jb // 2
            if tj < 0:
                continue
            tb = jb % 2
            if jb == 0:
                qlo, qhi = 0, 1
            elif jb == 5:
                qlo, qhi = 2, 3
            else:
                qlo, qhi = 0, 3
            dlo = qlo + 8 - 2 * jb
            blocks.append((jb, tj % self.NR, tb, qlo, qhi, dlo))
        for m in range(8):
            psO = self.psF[:, 6, 0:T]
            psD = self.psF[:, 1, 0:T]
            okey = ("ps", 6)
            dkey = ("ps", 1)
            for bi, (jb, rgj, tb, qlo, qhi, dlo) in enumerate(blocks):
                nq = (qhi - qlo + 1) * 64
                c0, c1 = qlo * 64, (qhi + 1) * 64
                bc = self.ptr
                self.ptr += 1
                for hh in range(2):
                    head = 2 * m + hh
                    r4 = (2 * bc + hh) % 4
                    sbank = 2 + 2 * hh + bc % 2
                    psS = self.psF[:, sbank, 0:nq]
                    sk = ("ps", sbank)
                    KTb = self.KT[rgj]
                    VVb = self.VV[rgj]
                    vkeys = [("VV", rgj, tb, (head * 64) // 256)]
                    op("pe", lambda e, hh=hh, m=m, tb=tb, c0=c0, c1=c1, psS=psS, KTb=KTb: e.matmul(
                        psS, lhsT=KTb[hh * 64:(hh + 1) * 64, m, tb * P:(tb + 1) * P], rhs=self.QT[hh * 64:(hh + 1) * 64, m, c0:c1],
                        start=True, stop=True), r=[("KT", rgj, m), ("QT", m)], w=[sk])
                    sc = self.sc[r4][:, 0:nq]
                    PT = self.PT[r4][:, 0:nq]
                    op("dve", lambda e, psS=psS, sc=sc, head=head, dlo=dlo, nq=nq: e.scalar_tensor_tensor(
                        out=sc, in0=psS, scalar=0.125, in1=self.bias[:, head, dlo * 64: dlo * 64 + nq], op0=ALU.mult, op1=ALU.add),
                       r=[sk, ("bias",)], w=[("sc", r4)])
                    op("act", lambda e, sc=sc, PT=PT: e.activation(out=PT, in_=sc, func=AF.Exp), r=[("sc", r4)], w=[("PT", r4)])
                    fst = (bi == 0)
                    lst = (bi == len(blocks) - 1)
                    op("pe", lambda e, hh=hh, head=head, tb=tb, c0=c0, c1=c1, VVb=VVb, PT=PT, fst=fst, lst=lst: e.matmul(
                        psO[hh * 64:(hh + 1) * 64, c0:c1], lhsT=VVb[:, tb, head * 64:(head + 1) * 64], rhs=PT, start=fst, stop=lst,
                        skip_group_check=True),
                       r=vkeys + [("PT", r4)], w=[okey])
                    op("pe", lambda e, hh=hh, c0=c0, c1=c1, PT=PT, fst=fst, lst=lst: e.matmul(
                        psD[hh * 64:(hh + 1) * 64, c0:c1], lhsT=self.ones[:, 0:64], rhs=PT, start=fst, stop=lst, skip_group_check=True),
                       r=[("cb",), ("PT", r4)], w=[dkey])
            rden = self.rden[m % 2]
            op("dve", lambda e, rden=rden, psD=psD: e.reciprocal(out=rden, in_=psD), r=[dkey], w=[("rden", m % 2)])
            op("dve", lambda e, m=m, rden=rden, psO=psO: e.tensor_tensor(out=self.on[:, m, :], in0=psO, in1=rden, op=ALU.mult),
               r=[okey, ("rden", m % 2)], w=[("on", m)])
        self.out_proj_residual(self.wo, "wo", 8, self.on, "on", h, hk, self.pcol(L, "g1"))


_SHARED = ("mod_w", "kv_mod_w", "ffn_w_in", "ffn_w_out", "a_w_in", "a_w_out", "kv_w", "b_w_q", "b_w_o")


def make_in_map(inp, b, S=SEQ, biasblk=None):
    m = {k: np.ascontiguousarray(np.asarray(inp[k], np.float32)) for k in _SHARED}
    m["xT"] = np.ascontiguousarray(np.asarray(inp["x"][b][:S], np.float32).T)
    m["vecs"] = _build_vecs(inp, b)
    m["biasblk"] = biasblk if biasblk is not None else _build_bias_blocks(inp["b_rel_bias"])
    return m


def kernel(**inputs):
    inp = {k: np.asarray(v) for k, v in inputs.items()}
    bld = Builder(S=SEQ, phases=ALL_PHASES)
    nc = bld.build()
    bb = _build_bias_blocks(inp["b_rel_bias"])
    shared = {k: np.ascontiguousarray(np.asarray(inp[k], np.float32)) for k in _SHARED}
    in_maps = []
    for b in range(NB):
        m = dict(shared)
        m["xT"] = np.ascontiguousarray(np.asarray(inp["x"][b], np.float32).T)
        m["vecs"] = _build_vecs(inp, b)
        m["biasblk"] = bb
        in_maps.append(m)
    res = run_bass_kernel_spmd(nc, in_maps, core_ids=list(range(NB)))
    out = np.empty((NB, SEQ, D), np.float32)
    for b in range(NB):
        out[b] = res.results[b]["yT"].T
    return out
```

```python
import numpy as np
from contextlib import ExitStack

import concourse.bass as bass
import concourse.mybir as mybir
from concourse.bass_utils import run_bass_kernel_spmd

F32 = mybir.dt.float32
BF16 = mybir.dt.bfloat16
AF = mybir.ActivationFunctionType
ALU = mybir.AluOpType

D = 1024
P = 128
KD = 8
SEQ = 8192
NB = 8
FFN = 2816
NFC = FFN // P
A_IN = 4096
EPS = 1e-6
T = 256
NCH = T // 64
NPAIR = T // 128
NEG = -30000.0


class _Op:
    __slots__ = ("eng", "fn", "deps", "dma", "need", "sig", "idx")

    def __init__(self, eng, fn, deps, dma):
        self.eng = eng
        self.fn = fn
        self.deps = deps
        self.dma = dma
        self.need = False
        self.sig = None
        self.idx = -1


class Prog:
    ENGS = ("pe", "act", "dve", "pool", "sp")
    NDS = 32
    EPOCH = 50000

    def __init__(self, nc):
        self.nc = nc
        self.streams = {e: [] for e in self.ENGS}
        self.lastw = {}
        self.readers = {}
        self.pending = {e: [] for e in self.ENGS}
        self.dmas = []
        self.dmas_since_barrier = []
        self.nops = 0
        self.limit = None

    def op(self, eng, fn, r=(), w=(), dma=False):
        deps = []
        for k in r:
            lw = self.lastw.get(k)
            if lw is not None:
                deps.append(lw)
        for k in w:
            lw = self.lastw.get(k)
            if lw is not None:
                deps.append(lw)
            deps.extend(self.readers.get(k, ()))
        if self.pending[eng]:
            deps.extend(self.pending[eng])
            self.pending[eng] = []
        o = _Op(eng, fn, deps, dma)
        self.nops += 1
        if self.limit is not None and self.nops > self.limit:
            return o
        o.idx = len(self.streams[eng])
        self.streams[eng].append(o)
        for k in r:
            self.readers.setdefault(k, []).append(o)
        for k in w:
            self.lastw[k] = o
            self.readers[k] = []
        if dma:
            self.dmas.append(o)
            self.dmas_since_barrier.append(o)
        return o

    def barrier(self):
        b = [s[-1] for s in self.streams.values() if s]
        b.extend(self.dmas_since_barrier)
        self.dmas_since_barrier = []
        for e in self.ENGS:
            self.pending[e] = list(b)

    @staticmethod
    def _counts(dep, o):
        if dep is o:
            return False
        if dep.eng == "pe" and o.eng == "pe" and not dep.dma and not o.dma:
            return False
        return True

    def emit(self):
        nc = self.nc
        for s in self.streams.values():
            for o in s:
                for d in o.deps:
                    if self._counts(d, o):
                        d.need = True
        with ExitStack() as es:
            esems = {}
            for e in self.ENGS:
                n = sum(1 for o in self.streams[e] if o.need and not o.dma)
                ne = max(1, -(-n // self.EPOCH))
                esems[e] = [es.enter_context(nc.semaphore(f"s_{e}_{i}")) for i in range(ne)]
                cnt = 0
                for o in self.streams[e]:
                    if o.need and not o.dma:
                        o.sig = (esems[e][cnt // self.EPOCH], cnt % self.EPOCH + 1)
                        cnt += 1
            by_eng = {}
            for o in self.dmas:
                by_eng.setdefault(o.eng, []).append(o)
            for en, lst in by_eng.items():
                nds = min(self.NDS, len(lst))
                dsems = [es.enter_context(nc.semaphore(f"s_dma_{en}_{i}")) for i in range(nds)]
                for i, o in enumerate(lst):
                    o.sig = (dsems[i % nds], 16 * (i // nds + 1))
                    if i >= nds:
                        o.deps.append(lst[i - nds])
            final_waits = {}
            for o in self.dmas:
                final_waits[id(o.sig[0])] = o.sig
            engmap = {"pe": "tensor", "act": "scalar", "dve": "vector", "pool": "gpsimd", "sp": "sync"}
            block = es.enter_context(nc.Block())

            def make(ename):
                def body(e):
                    waited = {}
                    for o in self.streams[ename]:
                        for d in o.deps:
                            if not self._counts(d, o):
                                continue
                            sem, val = d.sig
                            if waited.get(id(sem), 0) >= val:
                                continue
                            e.wait_ge(sem, val)
                            waited[id(sem)] = val
                        ins = o.fn(e)
                        if o.dma:
                            ins.then_inc(o.sig[0], 16)
                        elif o.need:
                            ins.then_inc(o.sig[0], 1)
                    if ename == "sp":
                        for sem, val in final_waits.values():
                            if waited.get(id(sem), 0) < val:
                                e.wait_ge(sem, val)
                return body

            for ename in self.ENGS:
                getattr(block, engmap[ename])(make(ename))


def _pk(v):
    v = np.asarray(v, np.float32)
    return np.ascontiguousarray(v.reshape(-1, P).T)


class VecLayout:
    def __init__(self):
        self.off = {}
        self.n = 0

    def add(self, name, ncols):
        self.off[name] = (self.n, ncols)
        self.n += ncols


def _vec_layout():
    vl = VecLayout()
    vl.add("c", 8)
    for l in range(4):
        vl.add(f"modb{l}", 48)
        vl.add(f"nmix{l}", 8)
        vl.add(f"nffn{l}", 8)
    vl.add("alb0", 8)
    vl.add("alb1", 8)
    vl.add("aon0", 1)
    vl.add("aon1", 1)
    vl.add("kvn", 8)
    vl.add("kvmodb", 16)
    vl.add("fn", 8)
    vl.add("ident", 128)
    vl.add("tri", 64)
    vl.add("scanmask", T)
    return vl


VL = _vec_layout()


def _build_vecs(inp, b):
    v = np.zeros((P, VL.n), np.float32)

    def put(name, arr):
        o, n = VL.off[name]
        assert arr.shape == (P, n), (name, arr.shape, n)
        v[:, o:o + n] = arr

    put("c", _pk(inp["c"][b]))
    for l in range(4):
        put(f"modb{l}", _pk(inp["mod_b"][l]))
        put(f"nmix{l}", _pk(inp["norm_mix"][l]))
        put(f"nffn{l}", _pk(inp["norm_ffn"][l]))
    put("alb0", _pk(inp["a_lb"][0]))
    put("alb1", _pk(inp["a_lb"][1]))
    put("aon0", _pk(inp["a_out_norm"][0]))
    put("aon1", _pk(inp["a_out_norm"][1]))
    put("kvn", _pk(inp["kv_norm"]))
    put("kvmodb", _pk(inp["kv_mod_b"]))
    put("fn", _pk(inp["final_norm"]))
    put("ident", np.eye(P, dtype=np.float32))
    s = np.arange(P)[:, None] % 64
    t = np.arange(64)[None, :]
    put("tri", (s <= t).astype(np.float32))
    sm = np.ones((P, T), np.float32)
    sm[:, ::64] = 0.0
    put("scanmask", sm)
    return v


def _build_bias_blocks(rel_bias):
    rb = np.asarray(rel_bias, np.float32)
    p = np.arange(P)
    half = p // 64
    ki = p % 64
    qi = np.arange(64)
    out = np.full((2, P, 16, 10, 64), NEG, np.float32)
    for d1 in range(10):
        dl = d1 - half
        valid = (dl >= 0) & (dl <= 8)
        rel = np.clip(ki[:, None] - qi[None, :] - 64 * dl[:, None], -256, 63) + 256
        for l in range(2):
            g = rb[l][rel]
            g = np.transpose(g, (0, 2, 1))
            out[l, valid, :, d1, :] = g[valid]
    return np.ascontiguousarray(out.reshape(2, P, 16 * 640))


ALL_PHASES = ("pro", "m0", "f0", "m1", "f1", "m2", "f2", "m3", "f3")


class Builder:
    def __init__(self, S=SEQ, phases=ALL_PHASES, h_in=False, h_out=False):
        self.S = S
        self.NT = S // T
        self.phases = tuple(phases)
        self.h_in = h_in
        self.h_out = h_out
        self.nc = bass.Bass("TRN2", target_bir_lowering=False)
        self.pg = Prog(self.nc)
        self._snap = None
        self._applied = None
        self.pslot = 0
        self.pbuf = []
        self.pbanks = [0, 1]

    def op(self, eng, fn, **k):
        snap = self._snap
        if snap is None:
            return self.pg.op(eng, fn, **k)

        def fn2(e, fn=fn, snap=snap):
            if self._applied is not snap:
                self.__dict__.update(snap)
                self._applied = snap
            return fn(e)
        return self.pg.op(eng, fn2, **k)

    def freeze(self):
        self._snap = {k: v for k, v in self.__dict__.items() if k not in ("_snap", "_applied", "pg", "nc", "es")}

    def dram_in(self, name, shape, dt=F32):
        return self.nc.dram_tensor(name, list(shape), dt, kind="ExternalInput").ap()

    def pipe_add(self, mm, ev, ncols=T):
        self.pbuf.append((mm, ev, ncols))
        if len(self.pbuf) == 2:
            self.pipe_flush()

    def pipe_flush(self):
        if not self.pbuf:
            return
        bank = self.pbanks[self.pslot % len(self.pbanks)]
        self.pslot += 1
        key = ("ps", bank)
        aps = []
        off = 0
        for mm, ev, n in self.pbuf:
            ap = self.psF[:, bank, off:off + n]
            off += n
            aps.append(ap)
            mm(ap, key)
        for (mm, ev, n), ap in zip(self.pbuf, aps):
            ev(ap, key)
        self.pbuf = []

    def reset_alloc(self):
        self.aoff = 0

    def _alloc_words(self, nwords):
        nwords = (nwords + 7) // 8 * 8
        o = self.aoff
        self.aoff += nwords
        assert self.aoff <= self.RW, f"region overflow {self.aoff} > {self.RW}"
        return o

    def f32(self, n):
        o = self._alloc_words(n)
        return self.R[:, o:o + n]

    def bf(self, n):
        o = self._alloc_words((n + 1) // 2)
        return self.R[:, o:o + (n + 1) // 2].bitcast(BF16)[:, 0:n]

    def build(self):
        nc = self.nc
        S = self.S
        with ExitStack() as es:
            self.es = es
            self.xT = self.dram_in("xT", [D, S])
            self.vecs_d = self.dram_in("vecs", [P, VL.n])
            self.mod_w = self.dram_in("mod_w", [4, D, 6144])
            self.kv_mod_w = self.dram_in("kv_mod_w", [D, 2048])
            self.ffn_w_in = self.dram_in("ffn_w_in", [4, D, 2 * FFN])
            self.ffn_w_out = self.dram_in("ffn_w_out", [4, FFN, D])
            self.a_w_in = self.dram_in("a_w_in", [2, D, A_IN])
            self.a_w_out = self.dram_in("a_w_out", [2, D, D])
            self.kv_w = self.dram_in("kv_w", [D, 2048])
            self.b_w_q = self.dram_in("b_w_q", [2, D, D])
            self.b_w_o = self.dram_in("b_w_o", [2, D, D])
            self.biasblk = self.dram_in("biasblk", [2, P, 16 * 640])
            self.yT = nc.dram_tensor("yT", [D, S], F32, kind="ExternalOutput").ap()
            if self.h_in:
                self.H_in = self.dram_in("H_in", [D, S])
                self.KT_in = self.dram_in("KT_in", [D, S], BF16)
                self.VV_in = self.dram_in("VV_in", [S, D], BF16)
                self.par_in = self.dram_in("par_in", [P, 512])
            if self.h_out:
                self.H = nc.dram_tensor("H", [D, S], F32, kind="ExternalOutput").ap()
                self.KTs = nc.dram_tensor("KTs", [D, S], BF16, kind="ExternalOutput").ap()
                self.VVs = nc.dram_tensor("VVs", [S, D], BF16, kind="ExternalOutput").ap()
                self.par_out = nc.dram_tensor("par_out", [P, 512], F32, kind="ExternalOutput").ap()
            else:
                self.H = nc.dram_tensor("H", [D, S], F32).ap()
                self.KTs = nc.dram_tensor("KTs", [D, S], BF16).ap()
                self.VVs = nc.dram_tensor("VVs", [S, D], BF16).ap()

            self.vecs = es.enter_context(nc.sbuf_tensor("vecs_sb", [P, VL.n], F32))
            self.par = es.enter_context(nc.sbuf_tensor("par_sb", [P, 512], F32))
            self.cb = es.enter_context(nc.sbuf_tensor("cb_sb", [P, 512], BF16))
            self.state = es.enter_context(nc.sbuf_tensor("state_sb", [P, 8, 128], F32))
            self.cb2 = es.enter_context(nc.sbuf_tensor("cb2_sb", [P, 8], F32))
            self.RW = 49500
            self.R = es.enter_context(nc.sbuf_tensor("region", [P, self.RW], F32))
            self.psF = es.enter_context(nc.psum_tensor("psF", [P, 7, 512], F32))
            self.psB = es.enter_context(nc.psum_tensor("psB", [P, 1024], BF16))

            self.ident = self.cb[:, 0:128]
            self.ones = self.cb[:, 128:256]
            self.tri = self.cb[:, 256:320]
            self.zeros32 = self.cb[:, 320:352]

            self.freeze()
            self.setup_consts()
            ph = self.phases
            if "pro" in ph:
                self.prologue()
            else:
                self.load_params()
            first_src_is_x = "pro" in ph
            src_is_x = first_src_is_x
            mix_layers = {"m0": 0, "m1": 1, "m2": 2, "m3": 3}
            ffn_layers = {"f0": 0, "f1": 1, "f2": 2, "f3": 3}
            todo = [p for p in ph if p != "pro"]
            for i, p in enumerate(todo):
                self.pg.barrier()
                if src_is_x:
                    src = self.xT
                elif i == 0 and self.h_in:
                    src = self.H_in
                else:
                    src = self.H
                last = (p == "f3")
                dst = self.yT if last else self.H
                if p in ("m0", "m1"):
                    self.hgrn_phase(mix_layers[p], src, dst)
                elif p in ("m2", "m3"):
                    self.att_phase(mix_layers[p], src, dst, first=(i == 0 and self.h_in))
                else:
                    self.ffn_phase(ffn_layers[p], src, dst, final=last)
                src_is_x = False
            if self.h_out:
                self.pg.barrier()
                self.op("sp", lambda e: e.dma_start(out=self.par_out, in_=self.par[:]), r=[("par",)], w=[("par_out",)], dma=True)
            self.pg.emit()
        return nc

    def vcol(self, name):
        o, n = VL.off[name]
        return self.vecs[:, o:o + n]

    def setup_consts(self):
        op = self.op
        op("sp", lambda e: e.dma_start(out=self.vecs[:], in_=self.vecs_d), w=[("vecs",)], dma=True)
        op("dve", lambda e: e.tensor_copy(out=self.ident, in_=self.vcol("ident")), r=[("vecs",)], w=[("cb",)])
        op("dve", lambda e: e.memset(self.ones, 1.0), w=[("cb",)])
        op("dve", lambda e: e.memset(self.zeros32, 0.0), w=[("cb",)])
        if "pro" in self.phases:
            op("dve", lambda e: e.memset(self.par[:], 0.0), w=[("par",)])
        op("dve", lambda e: e.tensor_copy(out=self.tri, in_=self.vcol("tri")), r=[("vecs",)], w=[("cb",)])
        op("dve", lambda e: e.memset(self.state[:], 0.0), w=[("S", h) for h in range(8)])
        self.scanmask = self.vcol("scanmask")
        self.trif = self.vcol("tri")
        op("dve", lambda e: e.memset(self.cb2[:, 0:1], EPS), w=[("cb2",)])
        self.epsc = self.cb2[:, 0:1]

    def pcol(self, l, name):
        o = {"a1": 0, "b1": 8, "g1": 16, "a2": 24, "b2": 32, "g2": 40}[name]
        return self.par[:, 64 * l + o: 64 * l + o + 8]

    def hcol(self, l, name):
        o = {"s_a": 0, "ns_a": 8, "b_a": 16}[name]
        return self.par[:, 272 + 32 * l + o: 272 + 32 * l + o + 8]

    def load_params(self):
        self.op("sp", lambda e: e.dma_start(out=self.par[:], in_=self.par_in), w=[("par",)], dma=True)

    def prologue(self):
        op = self.op
        self.reset_alloc()
        self.freeze()
        cact = self.f32(8)
        op("act", lambda e: e.activation(out=cact, in_=self.vcol("c"), func=AF.Silu), r=[("vecs",)], w=[("cact",)])
        NPIECE = 1024
        wb = [self.f32(8 * NPIECE).rearrange("p (k n) -> p k n", k=8) for _ in range(2)]
        modsb = self.f32(64)
        piece = 0
        jobs = [(self.mod_w[l], 6144, l) for l in range(4)] + [(self.kv_mod_w, 2048, 4)]
        for wd, ncol, l in jobs:
            nm = ncol // P
            psm = self.psF[:, 0, 0:nm]
            for pc in range(ncol // NPIECE):
                buf = wb[piece % 2]
                bk = ("modw", piece % 2)
                piece += 1
                src = wd.rearrange("(k p) n -> p k n", p=P)[:, :, pc * NPIECE:(pc + 1) * NPIECE]
                op("sp", lambda e, buf=buf, src=src: e.dma_start(out=buf, in_=src), w=[bk], dma=True)
                for mm in range(NPIECE // P):
                    m = pc * (NPIECE // P) + mm
                    for k in range(8):
                        op("pe", lambda e, buf=buf, mm=mm, k=k, m=m, psm=psm: e.matmul(
                            psm[:, m:m + 1], lhsT=buf[:, k, mm * P:(mm + 1) * P], rhs=cact[:, k:k + 1],
                            start=(k == 0), stop=(k == 7)), r=[bk, ("cact",)], w=[("ps", 0)])
            if l < 4:
                op("dve", lambda e, psm=psm, l=l: e.tensor_tensor(out=modsb[:, 0:48], in0=psm, in1=self.vcol(f"modb{l}"), op=ALU.add),
                   r=[("ps", 0), ("vecs",)], w=[("modsb",)])
                for (nm_, ncolname, so, go, sh) in (("nmix", "a1", 8, 16, 0), ("nffn", "a2", 32, 40, 24)):
                    a = self.pcol(l, ncolname)
                    bcol = self.pcol(l, "b1" if ncolname == "a1" else "b2")
                    gcol = self.pcol(l, "g1" if ncolname == "a1" else "g2")
                    op("dve", lambda e, a=a, so=so, l=l, nm_=nm_: e.scalar_tensor_tensor(
                        out=a, in0=modsb[:, so:so + 8], scalar=1.0, in1=self.vcol(f"{nm_}{l}"), op0=ALU.add, op1=ALU.mult),
                       r=[("modsb",), ("vecs",)], w=[("par",)])
                    op("dve", lambda e, bcol=bcol, sh=sh: e.tensor_copy(out=bcol, in_=modsb[:, sh:sh + 8]), r=[("modsb",)], w=[("par",)])
                    op("dve", lambda e, gcol=gcol, go=go: e.tensor_copy(out=gcol, in_=modsb[:, go:go + 8]), r=[("modsb",)], w=[("par",)])
            else:
                op("dve", lambda e, psm=psm: e.tensor_tensor(out=modsb[:, 0:16], in0=psm, in1=self.vcol("kvmodb"), op=ALU.add),
                   r=[("ps", 0), ("vecs",)], w=[("modsb",)])
                op("dve", lambda e: e.scalar_tensor_tensor(out=self.par[:, 256:264], in0=modsb[:, 8:16], scalar=1.0, in1=self.vcol("kvn"),
                                                           op0=ALU.add, op1=ALU.mult), r=[("modsb",), ("vecs",)], w=[("par",)])
                op("dve", lambda e: e.tensor_copy(out=self.par[:, 264:272], in_=modsb[:, 0:8]), r=[("modsb",)], w=[("par",)])
        a0 = self.vcol("alb0")
        a1 = self.vcol("alb1")
        t = [self.f32(8) for _ in range(8)]
        mx, e0, e1, den, sm0, sm1, cs1, lbt = t
        seq = [
            ("dve", lambda e: e.tensor_tensor(out=mx, in0=a0, in1=a1, op=ALU.max)),
            ("dve", lambda e: e.tensor_tensor(out=e0, in0=a0, in1=mx, op=ALU.subtract)),
            ("dve", lambda e: e.tensor_tensor(out=e1, in0=a1, in1=mx, op=ALU.subtract)),
            ("act", lambda e: e.activation(out=e0, in_=e0, func=AF.Exp)),
            ("act", lambda e: e.activation(out=e1, in_=e1, func=AF.Exp)),
            ("dve", lambda e: e.tensor_tensor(out=den, in0=e0, in1=e1, op=ALU.add)),
            ("dve", lambda e: e.reciprocal(out=den, in_=den)),
            ("dve", lambda e: e.tensor_tensor(out=sm0, in0=e0, in1=den, op=ALU.mult)),
            ("dve", lambda e: e.tensor_tensor(out=sm1, in0=e1, in1=den, op=ALU.mult)),
            ("dve", lambda e: e.tensor_tensor(out=cs1, in0=sm0, in1=sm1, op=ALU.add)),
        ]
        for eng, fn in seq:
            op(eng, fn, r=[("vecs",), ("lbtmp",)], w=[("lbtmp",)])
        for l in range(2):
            cs = sm0 if l == 0 else cs1
            op("dve", lambda e, cs=cs: e.tensor_tensor(out=lbt, in0=cs, in1=sm0, op=ALU.subtract), r=[("lbtmp",)], w=[("lbtmp",)])
            op("dve", lambda e, l=l: e.tensor_scalar(out=self.hcol(l, "s_a"), in0=lbt, scalar1=-0.5, scalar2=0.5, op0=ALU.mult, op1=ALU.add),
               r=[("lbtmp",)], w=[("par",)])
            op("dve", lambda e, l=l: e.tensor_scalar(out=self.hcol(l, "ns_a"), in0=lbt, scalar1=0.5, scalar2=-0.5, op0=ALU.mult, op1=ALU.add),
               r=[("lbtmp",)], w=[("par",)])
            op("dve", lambda e, l=l: e.tensor_scalar(out=self.hcol(l, "b_a"), in0=lbt, scalar1=0.5, scalar2=0.5, op0=ALU.mult, op1=ALU.add),
               r=[("lbtmp",)], w=[("par",)])

    def tile_src(self, src, ti):
        return src.rearrange("(k p) t -> p k t", p=P)[:, :, ti * T:(ti + 1) * T]

    def load_h(self, src, ti, srckey):
        par = ti % 2
        h = self.h[par]
        self.op("sp", lambda e: e.dma_start(out=h, in_=self.tile_src(src, ti)),
                r=[(srckey, ti)], w=[("h", par, k) for k in range(8)], dma=True)

    def store_h(self, dst, ti, dstkey):
        par = ti % 2
        h = self.h[par]
        self.op("sp", lambda e: e.dma_start(out=self.tile_src(dst, ti), in_=h),
                r=[("h", par, k) for k in range(8)], w=[(dstkey, ti)], dma=True)

    def rstd(self, h, hk):
        op = self.op
        u = self.u
        uk = [("u", k) for k in range(8)]
        op("act", lambda e: e.activation(out=u.rearrange("p k t -> p (k t)"), in_=h.rearrange("p k t -> p (k t)"), func=AF.Square),
           r=hk, w=uk)

        def mm(ps, pk):
            for k in range(8):
                op("pe", lambda e, k=k: e.matmul(ps, lhsT=self.ones, rhs=u[:, k, :], start=(k == 0), stop=(k == 7)),
                   r=[uk[k], ("cb",)], w=[pk])

        def ev(ps, pk):
            op("act", lambda e: e.activation(out=self.lnt, in_=ps, func=AF.Ln, scale=1.0 / D, bias=self.epsc), r=[pk, ("cb2",)], w=[("lnt",)])
            op("act", lambda e: e.activation(out=self.rs, in_=self.lnt, func=AF.Exp, scale=-0.5), r=[("lnt",)], w=[("rs",)])

        self.pipe_flush()
        self.pipe_add(mm, ev)
        self.pipe_flush()

    def affine(self, h, hk, a, b, u, uname):
        op = self.op
        for k in range(8):
            tmp = self.tmp[k % 2]
            tk = ("tmp", k % 2)
            op("dve", lambda e, k=k, tmp=tmp: e.scalar_tensor_tensor(out=tmp, in0=h[:, k, :], scalar=a[:, k:k + 1], in1=self.rs,
                                                                     op0=ALU.mult, op1=ALU.mult),
               r=[hk[k], ("rs",), ("par",)], w=[tk])
            op("act", lambda e, k=k, tmp=tmp: e.activation(out=u[:, k, :], in_=tmp, func=AF.Identity, bias=b[:, k:k + 1], scale=1.0),
               r=[tk, ("par",)], w=[(uname, k)])

    def load_w(self, dst3, src2, nk, key):
        for k in range(nk):
            self.op("pool", lambda e, k=k: e.dma_start(out=dst3[:, k, :], in_=src2[k * P:(k + 1) * P, :]), w=[(key, k)], dma=True)

    def mm_fm(self, wt, wkey, col0, u, uname):
        def mm(ps, pk):
            for k in range(8):
                self.op("pe", lambda e, k=k: e.matmul(ps, lhsT=wt[:, k, col0:col0 + P], rhs=u[:, k, :], start=(k == 0), stop=(k == 7)),
                        r=[(wkey, k), (uname, k)], w=[pk])
        return mm

    def mm_tm(self, wt, wkey, col0, ncol, u, uname, tb):
        def mm(ps, pk):
            for k in range(8):
                self.op("pe", lambda e, k=k: e.matmul(ps, lhsT=u[:, k, tb * P:(tb + 1) * P], rhs=wt[:, k, col0:col0 + ncol],
                                                      start=(k == 0), stop=(k == 7)),
                        r=[(wkey, k), (uname, k)], w=[pk])
        return mm

    def ev_act(self, out, okey, func, scale=1.0):
        def ev(ps, pk):
            self.op("act", lambda e: e.activation(out=out, in_=ps, func=func, scale=scale), r=[pk], w=[okey])
        return ev

    def ev_dve_copy(self, out, okey):
        def ev(ps, pk):
            self.op("dve", lambda e: e.tensor_copy(out=out, in_=ps), r=[pk], w=[okey])
        return ev

    def out_proj_residual(self, wt, wkey, nk, x, xname, h, hk, g):
        op = self.op
        for m in range(8):
            def mm(ps, pk, m=m):
                for k in range(nk):
                    op("pe", lambda e, k=k: e.matmul(ps, lhsT=wt[:, k, m * P:(m + 1) * P], rhs=x[:, k, :],
                                                     start=(k == 0), stop=(k == nk - 1)),
                       r=[(wkey, k), (xname, k)], w=[pk])

            def ev(ps, pk, m=m):
                op("dve", lambda e: e.scalar_tensor_tensor(out=h[:, m, :], in0=ps, scalar=g[:, m:m + 1], in1=h[:, m, :],
                                                           op0=ALU.mult, op1=ALU.add),
                   r=[pk, hk[m], ("par",)], w=[hk[m]])
            self.pipe_add(mm, ev)
        self.pipe_flush()

    def hgrn_phase(self, L, src, dst):
        op = self.op
        self.reset_alloc()
        self.pbanks = [0, 1]
        self.win = self.bf(8 * A_IN).rearrange("p (k n) -> p k n", k=8)
        self.wout = self.bf(8 * D).rearrange("p (k n) -> p k n", k=8)
        self.u = self.bf(8 * T).rearrange("p (k t) -> p k t", k=8)
        self.q = self.bf(8 * T).rearrange("p (k t) -> p k t", k=8)
        self.g = self.bf(8 * T).rearrange("p (k t) -> p k t", k=8)
        self.on = self.bf(8 * T).rearrange("p (k t) -> p k t", k=8)
        self.v = self.bf(NPAIR * D).rearrange("p (b n) -> p b n", b=NPAIR)
        self.h = [self.f32(8 * T).rearrange("p (k t) -> p k t", k=8) for _ in range(2)]
        self.th = self.f32(8 * T).rearrange("p (k t) -> p k t", k=8)
        self.lnt = self.f32(T)
        self.rs = self.f32(T)
        self.tmp = [self.f32(T) for _ in range(2)]
        R2 = range(4)
        self.Ep = [self.bf(T) for _ in R2]
        self.Em = [self.bf(T) for _ in R2]
        self.qt = [self.bf(T) for _ in R2]
        self.kt = [self.bf(T) for _ in R2]
        self.ktT = [self.bf(NCH * P) for _ in R2]
        self.at = [self.bf(NCH * 64) for _ in R2]
        self.osq = [self.bf(T) for _ in R2]
        self.Ab = [self.bf(128) for _ in range(4)]
        self.X = [self.f32(NCH * 128) for _ in R2]
        self.lf = [self.f32(T) for _ in R2]
        self.bb = [self.f32(T) for _ in R2]
        self.bm = [self.f32(T) for _ in R2]
        self.kk = [self.f32(T) for _ in R2]
        self.lo = [self.f32(T) for _ in R2]
        self.rso = [self.f32(T) for _ in R2]
        self.t2 = [self.f32(T) for _ in R2]
        self.esm = [self.f32(3 * NCH) for _ in R2]
        self.esm2 = [self.f32(4 * NCH) for _ in R2]
        self.qta = [self.bf(T) for _ in R2]
        self.kta = [self.bf(T) for _ in R2]
        self.actr = 0
        self.freeze()

        self.load_w(self.win, self.a_w_in[L], 8, "win")
        self.load_w(self.wout, self.a_w_out[L], 8, "wout")
        for r_ in R2:
            op("pool", lambda e, r_=r_: e.memset(self.ktT[r_], 0.0), w=[("ktT", r_)])
            op("pool", lambda e, r_=r_: e.memset(self.at[r_], 0.0), w=[("at", r_)])
        op("dve", lambda e: e.memset(self.state[:], 0.0), r=[("S", h) for h in range(8)], w=[("S", h) for h in range(8)])
        srckey = "xT" if src is self.xT else ("Hin" if (self.h_in and src is self.H_in) else "H")
        dstkey = "H"
        self.load_h(src, 0, srckey)
        self.pre_tile(0, self.pcol(L, "a1"), self.pcol(L, "b1"))
        for ti in range(self.NT):
            if ti + 1 < self.NT:
                self.load_h(src, ti + 1, srckey)
            self.hgrn_tile(L, ti)
            if ti + 1 < self.NT:
                self.pre_tile(ti + 1, self.pcol(L, "a1"), self.pcol(L, "b1"))
            par = ti % 2
            self.out_proj_residual(self.wout, "wout", 8, self.on, "on", self.h[par], [("h", par, k) for k in range(8)], self.pcol(L, "g1"))
            self.store_h(dst, ti, dstkey)

    def pre_tile(self, ti, a, b, akv=None, bkv=None):
        par = ti % 2
        h = self.h[par]
        hk = [("h", par, k) for k in range(8)]
        self.rstd(h, hk)
        if akv is not None:
            self.affine(h, hk, akv, bkv, self.ukv, "ukv")
        self.affine(h, hk, a, b, self.u, "u")

    def hgrn_tile(self, L, ti):
        op = self.op
        par = ti % 2
        h = self.h[par]
        hk = [("h", par, k) for k in range(8)]
        u = self.u
        win = self.win
        s_a = self.hcol(L, "s_a")
        ns_a = self.hcol(L, "ns_a")
        b_a = self.hcol(L, "b_a")
        ogain = self.vcol(f"aon{L}")
        for hd in range(8):
            self.pipe_add(self.mm_fm(win, "win", hd * P, u, "u"), self.ev_act(self.q[:, hd, :], ("q", hd), AF.Silu))
            self.pipe_add(self.mm_fm(win, "win", 1024 + hd * P, u, "u"), self.ev_act(self.th[:, hd, :], ("th", hd), AF.Tanh, 0.5))
        for hd in range(8):
            self.pipe_add(self.mm_fm(win, "win", 3072 + hd * P, u, "u"), self.ev_act(self.g[:, hd, :], ("g", hd), AF.Silu))
        for tb in range(NPAIR):
            for nq in range(4):
                self.pipe_add(self.mm_tm(win, "win", 2048 + nq * 256, 256, u, "u", tb),
                              self.ev_dve_copy(self.v[:, tb, nq * 256:(nq + 1) * 256], ("v", tb, nq)))
        self.pipe_flush()
        def head_gen(hd):
            r = hd % 4
            lf, bb, bm, kk = self.lf[r], self.bb[r], self.bm[r], self.kk[r]
            Ep, Em, qt, kt, ktT, at = self.Ep[r], self.Em[r], self.qt[r], self.kt[r], self.ktT[r], self.at[r]
            esm = self.esm[r]
            X = self.X[r]
            emid, elast, elm = esm[:, 0:NCH], esm[:, NCH:2 * NCH], esm[:, 2 * NCH:3 * NCH]
            th = self.th[:, hd, :]
            op("act", lambda e, hd=hd, lf=lf, th=th: e.activation(out=lf, in_=th, func=AF.Ln, scale=s_a[:, hd:hd + 1], bias=b_a[:, hd:hd + 1]),
               r=[("th", hd), ("par",)], w=[("lf", r)])
            op("dve", lambda e, hd=hd, kk=kk, th=th: e.tensor_scalar(out=kk, in0=th, scalar1=ns_a[:, hd:hd + 1], scalar2=s_a[:, hd:hd + 1],
                                                                     op0=ALU.mult, op1=ALU.add),
               r=[("th", hd), ("par",)], w=[("kk", r)])
            op("dve", lambda e, lf=lf, bb=bb: e.tensor_tensor_scan(out=bb, data0=self.scanmask, data1=lf, initial=0.0, op0=ALU.mult, op1=ALU.add),
               r=[("lf", r), ("vecs",)], w=[("bb", r)])
            bb3 = bb.rearrange("p (c t) -> p c t", t=64)
            bm3 = bm.rearrange("p (c t) -> p c t", t=64)
            op("dve", lambda e, bb3=bb3, bm3=bm3: e.tensor_tensor(out=bm3, in0=bb3, in1=bb3[:, :, 31:32].to_broadcast([P, NCH, 64]), op=ALU.subtract),
               r=[("bb", r)], w=[("bm", r)])
            op("act", lambda e, Ep=Ep, bm=bm: e.activation(out=Ep, in_=bm, func=AF.Exp), r=[("bm", r)], w=[("Ep", r)])
            op("act", lambda e, Em=Em, bm=bm: e.activation(out=Em, in_=bm, func=AF.Exp, scale=-1.0), r=[("bm", r)], w=[("Em", r)])
            op("act", lambda e, emid=emid, bb3=bb3: e.activation(out=emid, in_=bb3[:, :, 31], func=AF.Exp), r=[("bb", r)], w=[("esm", r)])
            op("act", lambda e, elast=elast, bb3=bb3: e.activation(out=elast, in_=bb3[:, :, 63], func=AF.Exp), r=[("bb", r)], w=[("esm", r)])
            op("act", lambda e, elm=elm, bm3=bm3: e.activation(out=elm, in_=bm3[:, :, 63], func=AF.Exp), r=[("bm", r)], w=[("esm", r)])
            op("dve", lambda e, hd=hd, qt=qt, Ep=Ep: e.tensor_tensor(out=qt, in0=self.q[:, hd, :], in1=Ep, op=ALU.mult),
               r=[("q", hd), ("Ep", r)], w=[("qt", r)])
            op("dve", lambda e, kt=kt, kk=kk, Em=Em: e.tensor_tensor(out=kt, in0=kk, in1=Em, op=ALU.mult),
               r=[("kk", r), ("Em", r)], w=[("kt", r)])
            yield
            pst = self.psB[:, 0:T]
            for p_ in range(NPAIR):
                op("pe", lambda e, p_=p_, kt=kt: e.transpose(out=pst[:, p_ * P:(p_ + 1) * P], in_=kt[:, p_ * P:(p_ + 1) * P], identity=self.ident),
                   r=[("kt", r), ("cb",)], w=[("psB",)])
            for hf in range(2):
                op("dve", lambda e, ktT=ktT, hf=hf: e.tensor_copy(
                    out=ktT.rearrange("p (a b d) -> p a b d", b=2, d=P)[hf * 64:(hf + 1) * 64, :, hf, :],
                    in_=pst.rearrange("p (a d) -> p a d", d=P)[hf * 64:(hf + 1) * 64, :, :]), r=[("psB",)], w=[("ktT", r)])
            bmid4 = bm.rearrange("p (j t) -> p j t", t=32)[:, :, 15]
            sa, sb = self.esm2[r][:, 0:2 * NCH], self.esm2[r][:, 2 * NCH:4 * NCH]
            op("act", lambda e, sa=sa, bmid4=bmid4: e.activation(out=sa, in_=bmid4, func=AF.Exp, scale=-1.0), r=[("bm", r)], w=[("esm2", r)])
            op("act", lambda e, sb=sb, bmid4=bmid4: e.activation(out=sb, in_=bmid4, func=AF.Exp), r=[("bm", r)], w=[("esm2", r)])
            qta, kta = self.qta[r], self.kta[r]
            op("dve", lambda e, qta=qta, qt=qt, sa=sa: e.tensor_tensor(
                out=qta.rearrange("p (j t) -> p j t", t=32), in0=qt.rearrange("p (j t) -> p j t", t=32),
                in1=sa.rearrange("p (j o) -> p j o", o=1).to_broadcast([P, 2 * NCH, 32]), op=ALU.mult),
               r=[("qt", r), ("esm2", r)], w=[("qta", r)])
            op("dve", lambda e, kta=kta, kt=kt, sb=sb: e.tensor_tensor(
                out=kta.rearrange("p (j t) -> p j t", t=32), in0=kt.rearrange("p (j t) -> p j t", t=32),
                in1=sb.rearrange("p (j o) -> p j o", o=1).to_broadcast([P, 2 * NCH, 32]), op=ALU.mult),
               r=[("kt", r), ("esm2", r)], w=[("kta", r)])
            psat = self.psF[:, 2, 0:NPAIR * 64]
            for c in range(NCH):
                p_, hf = c // 2, c % 2
                rb, cb_, t0 = hf * 64, p_ * 64, c * 64
                quads = [
                    (rb, cb_, kta[:, t0:t0 + 32], qta[:, t0:t0 + 32]),
                    (rb, cb_ + 32, kt[:, t0:t0 + 32], qt[:, t0 + 32:t0 + 64]),
                    (rb + 32, cb_ + 32, kta[:, t0 + 32:t0 + 64], qta[:, t0 + 32:t0 + 64]),
                    (rb + 32, cb_, self.zeros32, qt[:, t0:t0 + 32]),
                ]
                for (r0, c0_, l_, r_) in quads:
                    op("pe", lambda e, r0=r0, c0_=c0_, l_=l_, r_=r_: e.matmul(
                        psat[r0:r0 + 32, c0_:c0_ + 32], lhsT=l_, rhs=r_, start=True, stop=True, tile_position=(0, r0)),
                       r=[("kt", r), ("qt", r), ("kta", r), ("qta", r), ("cb",)], w=[("ps", 2)])
            for hf in range(2):
                op("dve", lambda e, at=at, hf=hf: e.tensor_tensor(
                    out=at.rearrange("p (a b t) -> p a b t", b=2, t=64)[hf * 64:(hf + 1) * 64, :, hf, :],
                    in0=psat.rearrange("p (a t) -> p a t", t=64)[hf * 64:(hf + 1) * 64, :, :],
                    in1=self.trif.rearrange("p (a t) -> p a t", a=1)[hf * 64:(hf + 1) * 64, :, :].to_broadcast([64, NPAIR, 64]), op=ALU.mult),
                   r=[("ps", 2), ("vecs",)], w=[("at", r)])
            dbank = 3 + hd % 2
            vsls = []
            for c in range(NCH):
                p_, hf = c // 2, c % 2
                vsl = self.v[:, p_, hd * P:(hd + 1) * P]
                vsls.append(vsl)
                vkeys = [("v", p_, (hd * P) // 256)]
                psd = self.psF[:, dbank, c * P:(c + 1) * P]
                op("pe", lambda e, c=c, psd=psd, ktT=ktT, vsl=vsl: e.matmul(
                    psd, lhsT=ktT[:, c * P:(c + 1) * P], rhs=vsl, start=True, stop=True),
                   r=vkeys + [("ktT", r)], w=[("ps", dbank)])
            for c in range(NCH):
                psd = self.psF[:, dbank, c * P:(c + 1) * P]
                op("act", lambda e, c=c, psd=psd, X=X, elm=elm: e.activation(out=X[:, c * P:(c + 1) * P], in_=psd, func=AF.Copy, scale=elm[:, c:c + 1]),
                   r=[("ps", dbank), ("esm", r)], w=[("X", r, c)])
            yield
            obank = 5 + hd % 2
            pso = self.psF[:, obank, 0:T]
            S_hd = self.state[:, hd, :]
            for c in range(NCH):
                p_, hf = c // 2, c % 2
                ar = self.actr % 4
                self.actr += 1
                Ab = self.Ab[ar]
                vsl = vsls[c]
                vkeys = [("v", p_, (hd * P) // 256)]
                op("act", lambda e, c=c, Ab=Ab, S_hd=S_hd, emid=emid: e.activation(out=Ab, in_=S_hd, func=AF.Copy, scale=emid[:, c:c + 1]),
                   r=[("S", hd), ("esm", r)], w=[("Ab", ar)])
                op("pe", lambda e, c=c, vsl=vsl, at=at: e.matmul(
                    pso[:, c * 64:(c + 1) * 64], lhsT=vsl, rhs=at[:, c * 64:(c + 1) * 64], start=True, stop=False),
                   r=vkeys + [("at", r)], w=[("ps", obank)])
                op("pe", lambda e, c=c, Ab=Ab, qt=qt: e.matmul(
                    pso[:, c * 64:(c + 1) * 64], lhsT=Ab, rhs=qt[:, c * 64:(c + 1) * 64], start=False, stop=True),
                   r=[("Ab", ar), ("qt", r)], w=[("ps", obank)])
                op("dve", lambda e, c=c, S_hd=S_hd, X=X, elast=elast: e.scalar_tensor_tensor(
                    out=S_hd, in0=S_hd, scalar=elast[:, c:c + 1], in1=X[:, c * P:(c + 1) * P], op0=ALU.mult, op1=ALU.add),
                   r=[("X", r, c), ("S", hd), ("esm", r)], w=[("S", hd)])
            yield
            osq, lo, rso, t2 = self.osq[r], self.lo[r], self.rso[r], self.t2[r]
            op("act", lambda e, osq=osq: e.activation(out=osq, in_=pso, func=AF.Square), r=[("ps", obank)], w=[("osq", r)])

            def mm(ps, pk, osq=osq):
                op("pe", lambda e: e.matmul(ps, lhsT=self.ones, rhs=osq, start=True, stop=True), r=[("osq", r), ("cb",)], w=[pk])

            def ev(ps, pk, lo=lo, rso=rso, r=r):
                op("act", lambda e: e.activation(out=lo, in_=ps, func=AF.Ln, scale=1.0 / 128.0, bias=self.epsc), r=[pk, ("cb2",)], w=[("lo", r)])
                op("act", lambda e: e.activation(out=rso, in_=lo, func=AF.Exp, scale=-0.5), r=[("lo", r)], w=[("rso", r)])
            self.pipe_add(mm, ev)
            self.pipe_flush()
            op("dve", lambda e, t2=t2, rso=rso: e.scalar_tensor_tensor(out=t2, in0=pso, scalar=ogain[:, 0:1], in1=rso, op0=ALU.mult, op1=ALU.mult),
               r=[("ps", obank), ("rso", r), ("vecs",)], w=[("t2", r)])
            op("pool", lambda e, hd=hd, t2=t2: e.tensor_tensor(out=self.on[:, hd, :], in0=t2, in1=self.g[:, hd, :], op=ALU.mult),
               r=[("t2", r), ("g", hd)], w=[("on", hd)])
        gens = [head_gen(hd) for hd in range(8)]
        NST = 4
        for s_ in range(8 + NST - 1):
            for st in range(NST - 1, -1, -1):
                hd_ = s_ - st
                if 0 <= hd_ < 8:
                    next(gens[hd_], None)

    def ffn_phase(self, L, src, dst, final):
        op = self.op
        self.reset_alloc()
        self.pbanks = [0, 1, 2, 3, 4, 5, 6]
        self.fwin = self.bf(8 * 2 * FFN).rearrange("p (k n) -> p k n", k=8)
        self.fwout = self.bf(NFC * D).rearrange("p (k n) -> p k n", k=NFC)
        self.u = self.bf(8 * T).rearrange("p (k t) -> p k t", k=8)
        self.actb = self.bf(NFC * T).rearrange("p (k t) -> p k t", k=NFC)
        self.sg = [self.bf(T) for _ in range(4)]
        self.h = [self.f32(8 * T).rearrange("p (k t) -> p k t", k=8) for _ in range(2)]
        self.lnt = self.f32(T)
        self.rs = self.f32(T)
        self.tmp = [self.f32(T) for _ in range(2)]
        self.freeze()
        self.load_w(self.fwin, self.ffn_w_in[L], 8, "fwin")
        self.load_w(self.fwout, self.ffn_w_out[L], NFC, "fwout")
        srckey = "Hin" if (self.h_in and src is self.H_in) else "H"
        dstkey = "yT" if final else "H"
        self.load_h(src, 0, srckey)
        self.pre_tile(0, self.pcol(L, "a2"), self.pcol(L, "b2"))
        for ti in range(self.NT):
            if ti + 1 < self.NT:
                self.load_h(src, ti + 1, srckey)
            self.ffn_tile(L, ti, final)
            if ti + 1 < self.NT and not final:
                self.pre_tile(ti + 1, self.pcol(L, "a2"), self.pcol(L, "b2"))
            self.ffn_tail(L, ti, final)
            if ti + 1 < self.NT and final:
                self.pre_tile(ti + 1, self.pcol(L, "a2"), self.pcol(L, "b2"))
            self.store_h(dst, ti, dstkey)

    def ffn_tile(self, L, ti, final):
        op = self.op
        par = ti % 2
        h = self.h[par]
        hk = [("h", par, k) for k in range(8)]
        u = self.u
        for m in range(NFC):
            sg = self.sg[m % 4]
            sk = ("sg", m % 4)
            def ev_g(ps, pk, sg=sg, sk=sk):
                op("act", lambda e: e.activation(out=sg, in_=ps, func=AF.Silu), r=[pk], w=[sk])

            def ev_u(ps, pk, sg=sg, sk=sk, m=m):
                op("dve", lambda e: e.tensor_tensor(out=self.actb[:, m, :], in0=sg, in1=ps, op=ALU.mult), r=[sk, pk], w=[("actb", m)])
            self.pipe_flush()
            self.pipe_add(self.mm_fm(self.fwin, "fwin", m * P, u, "u"), ev_g)
            self.pipe_add(self.mm_fm(self.fwin, "fwin", FFN + m * P, u, "u"), ev_u)
        self.pipe_flush()

    def ffn_tail(self, L, ti, final):
        op = self.op
        par = ti % 2
        h = self.h[par]
        hk = [("h", par, k) for k in range(8)]
        self.out_proj_residual(self.fwout, "fwout", NFC, self.actb, "actb", h, hk, self.pcol(L, "g2"))
        if final:
            self.rstd(h, hk)
            fn = self.vcol("fn")
            for k in range(8):
                op("dve", lambda e, k=k: e.scalar_tensor_tensor(out=h[:, k, :], in0=h[:, k, :], scalar=fn[:, k:k + 1], in1=self.rs,
                                                                op0=ALU.mult, op1=ALU.mult),
                   r=[hk[k], ("rs",), ("vecs",)], w=[hk[k]])

    def att_phase(self, L, src, dst, first):
        op = self.op
        j = L - 2
        self.reset_alloc()
        self.pbanks = [0, 1]
        self.wq = self.bf(8 * D).rearrange("p (k n) -> p k n", k=8)
        self.wo = self.bf(8 * D).rearrange("p (k n) -> p k n", k=8)
        if L == 2:
            self.kvw = self.bf(8 * 2 * D).rearrange("p (k n) -> p k n", k=8)
            self.ukv = self.bf(8 * T).rearrange("p (k t) -> p k t", k=8)
        self.u = self.bf(8 * T).rearrange("p (k t) -> p k t", k=8)
        self.QT = self.bf(8 * T).rearrange("p (k t) -> p k t", k=8)
        self.on = self.bf(8 * T).rearrange("p (k t) -> p k t", k=8)
        self.NR = 3 if L == 2 else 4
        self.KT = [self.bf(8 * T).rearrange("p (k t) -> p k t", k=8) for _ in range(self.NR)]
        self.VV = [self.bf(NPAIR * D).rearrange("p (b n) -> p b n", b=NPAIR) for _ in range(self.NR)]
        self.PT = [self.bf(T) for _ in range(4)]
        self.h = [self.f32(8 * T).rearrange("p (k t) -> p k t", k=8) for _ in range(2)]
        self.bias = self.f32(16 * 640).rearrange("p (h n) -> p h n", h=16)
        self.lnt = self.f32(T)
        self.rs = self.f32(T)
        self.tmp = [self.f32(T) for _ in range(2)]
        self.sc = [self.f32(T) for _ in range(4)]
        self.rden = [self.f32(T) for _ in range(2)]
        self.ptr = 0
        self.kv_from_in = (L == 3 and first)
        self.freeze()
        self.load_w(self.wq, self.b_w_q[j], 8, "wq")
        if L == 2:
            self.load_w(self.kvw, self.kv_w, 8, "kvw")
        self.load_w(self.wo, self.b_w_o[j], 8, "wo")
        op("sp", lambda e: e.dma_start(out=self.bias.rearrange("p h n -> p (h n)"), in_=self.biasblk[j]), w=[("bias",)], dma=True)
        srckey = "Hin" if (self.h_in and src is self.H_in) else "H"
        self.kv_from_in = (L == 3 and first)
        self.load_h(src, 0, srckey)
        if L == 3:
            self.load_kv(0)
        kvp = (self.par[:, 256:264], self.par[:, 264:272]) if L == 2 else (None, None)
        self.pre_tile(0, self.pcol(L, "a1"), self.pcol(L, "b1"), *kvp)
        for ti in range(self.NT):
            if ti + 1 < self.NT:
                self.load_h(src, ti + 1, srckey)
                if L == 3:
                    self.load_kv(ti + 1)
            self.att_tile(L, ti)
            if ti + 1 < self.NT and L == 3:
                self.pre_tile(ti + 1, self.pcol(L, "a1"), self.pcol(L, "b1"), *kvp)
            par = ti % 2
            self.out_proj_residual(self.wo, "wo", 8, self.on, "on", self.h[par], [("h", par, k) for k in range(8)], self.pcol(L, "g1"))
            if ti + 1 < self.NT and L == 2:
                self.pre_tile(ti + 1, self.pcol(L, "a1"), self.pcol(L, "b1"), *kvp)
            self.store_h(dst, ti, "H")

    def kt_dram(self, base, ti):
        return base.rearrange("(k p) t -> p k t", p=P)[:, :, ti * T:(ti + 1) * T]

    def vv_dram(self, base, ti):
        return base[ti * T:(ti + 1) * T, :].rearrange("(b p) n -> p b n", p=P)

    def load_kv(self, ti):
        rg = ti % self.NR
        kt_src = self.KT_in if self.kv_from_in else self.KTs
        vv_src = self.VV_in if self.kv_from_in else self.VVs
        self.op("sp", lambda e: e.dma_start(out=self.KT[rg], in_=self.kt_dram(kt_src, ti)),
                r=[("KTs", ti)], w=[("KT", rg, m) for m in range(8)], dma=True)
        self.op("sp", lambda e: e.dma_start(out=self.VV[rg], in_=self.vv_dram(vv_src, ti)),
                r=[("VVs", ti)], w=[("VV", rg, b, nq) for b in range(NPAIR) for nq in range(4)], dma=True)

    def att_tile(self, L, ti):
        op = self.op
        par = ti % 2
        h = self.h[par]
        hk = [("h", par, k) for k in range(8)]
        u = self.u
        rg = ti % self.NR
        if L == 2:
            KT, VV = self.KT[rg], self.VV[rg]
            for m in range(8):
                self.pipe_add(self.mm_fm(self.kvw, "kvw", m * P, self.ukv, "ukv"), self.ev_act(KT[:, m, :], ("KT", rg, m), AF.Copy))
            for tb in range(NPAIR):
                for nq in range(4):
                    self.pipe_add(self.mm_tm(self.kvw, "kvw", D + nq * 256, 256, self.ukv, "ukv", tb),
                                  self.ev_dve_copy(VV[:, tb, nq * 256:(nq + 1) * 256], ("VV", rg, tb, nq)))
            self.pipe_flush()
            op("sp", lambda e, KT=KT: e.dma_start(out=self.kt_dram(self.KTs, ti), in_=KT),
               r=[("KT", rg, m) for m in range(8)], w=[("KTs", ti)], dma=True)
            op("sp", lambda e, VV=VV: e.dma_start(out=self.vv_dram(self.VVs, ti), in_=VV),
               r=[("VV", rg, b, nq) for b in range(NPAIR) for nq in range(4)], w=[("VVs", ti)], dma=True)
        for m in range(8):
            self.pipe_add(self.mm_fm(self.wq, "wq", m * P, u, "u"), self.ev_act(self.QT[:, m, :], ("QT", m), AF.Copy))
        self.pipe_flush()
        blocks = []
        for jb in range(6):
            tj = ti - 2 + jb // 2
            if tj < 0:
                continue
            tb = jb % 2
            if jb == 0:
                qlo, qhi = 0, 1
            elif jb == 5:
                qlo, qhi = 2, 3
            else:
                qlo, qhi = 0, 3
            dlo = qlo + 8 - 2 * jb
            blocks.append((jb, tj % self.NR, tb, qlo, qhi, dlo))
        DEPTH = 2
        pending = []
        psO = self.psF[:, 6, 0:T]
        psD = self.psF[:, 1, 0:T]
        okey = ("ps", 6)
        dkey = ("ps", 1)

        def finalize(m):
            rden = self.rden[m % 2]
            op("dve", lambda e: e.reciprocal(out=rden, in_=psD), r=[dkey], w=[("rden", m % 2)])
            op("dve", lambda e: e.tensor_tensor(out=self.on[:, m, :], in0=psO, in1=rden, op=ALU.mult),
               r=[okey, ("rden", m % 2)], w=[("on", m)])

        for m in range(8):
            for bi, (jb, rgj, tb, qlo, qhi, dlo) in enumerate(blocks):
                nq = (qhi - qlo + 1) * 64
                c0, c1 = qlo * 64, (qhi + 1) * 64
                bc = self.ptr
                self.ptr += 1
                for hh in range(2):
                    head = 2 * m + hh
                    r4 = (2 * bc + hh) % 4
                    sbank = 2 + 2 * hh + bc % 2
                    psS = self.psF[:, sbank, 0:nq]
                    sk = ("ps", sbank)
                    KTb = self.KT[rgj]
                    VVb = self.VV[rgj]
                    vkeys = [("VV", rgj, tb, (head * 64) // 256)]
                    op("pe", lambda e, hh=hh, m=m, tb=tb, c0=c0, c1=c1, psS=psS, KTb=KTb: e.matmul(
                        psS, lhsT=KTb[hh * 64:(hh + 1) * 64, m, tb * P:(tb + 1) * P], rhs=self.QT[hh * 64:(hh + 1) * 64, m, c0:c1],
                        start=True, stop=True), r=[("KT", rgj, m), ("QT", m)], w=[sk])
                    sc = self.sc[r4][:, 0:nq]
                    PT = self.PT[r4][:, 0:nq]
                    op("dve", lambda e, psS=psS, sc=sc, head=head, dlo=dlo, nq=nq: e.scalar_tensor_tensor(
                        out=sc, in0=psS, scalar=0.125, in1=self.bias[:, head, dlo * 64: dlo * 64 + nq], op0=ALU.mult, op1=ALU.add),
                       r=[sk, ("bias",)], w=[("sc", r4)])
                    op("act", lambda e, sc=sc, PT=PT: e.activation(out=PT, in_=sc, func=AF.Exp), r=[("sc", r4)], w=[("PT", r4)])
                    fst = (bi == 0)
                    lst = (bi == len(blocks) - 1)

                    def back(hh=hh, head=head, tb=tb, c0=c0, c1=c1, VVb=VVb, PT=PT, fst=fst, lst=lst, vkeys=vkeys, r4=r4, m=m):
                        op("pe", lambda e: e.matmul(
                            psO[hh * 64:(hh + 1) * 64, c0:c1], lhsT=VVb[:, tb, head * 64:(head + 1) * 64], rhs=PT, start=fst, stop=lst,
                            skip_group_check=True),
                           r=vkeys + [("PT", r4)], w=[okey])
                        op("pe", lambda e: e.matmul(
                            psD[hh * 64:(hh + 1) * 64, c0:c1], lhsT=self.ones[:, 0:64], rhs=PT, start=fst, stop=lst, skip_group_check=True),
                           r=[("cb",), ("PT", r4)], w=[dkey])
                        if lst and hh == 1:
                            finalize(m)
                    pending.append(back)
                    if len(pending) > DEPTH:
                        pending.pop(0)()
        while pending:
            pending.pop(0)()


_SHARED = ("mod_w", "kv_mod_w", "ffn_w_in", "ffn_w_out", "a_w_in", "a_w_out", "kv_w", "b_w_q", "b_w_o")


def make_in_map(inp, b, S=SEQ, biasblk=None):
    m = {k: np.ascontiguousarray(np.asarray(inp[k], np.float32)) for k in _SHARED}
    m["xT"] = np.ascontiguousarray(np.asarray(inp["x"][b][:S], np.float32).T)
    m["vecs"] = _build_vecs(inp, b)
    m["biasblk"] = biasblk if biasblk is not None else _build_bias_blocks(inp["b_rel_bias"])
    return m


def kernel(**inputs):
    inp = {k: np.asarray(v) for k, v in inputs.items()}
    bld = Builder(S=SEQ, phases=ALL_PHASES)
    nc = bld.build()
    bb = _build_bias_blocks(inp["b_rel_bias"])
    shared = {k: np.ascontiguousarray(np.asarray(inp[k], np.float32)) for k in _SHARED}
    in_maps = []
    for b in range(NB):
        m = dict(shared)
        m["xT"] = np.ascontiguousarray(np.asarray(inp["x"][b], np.float32).T)
        m["vecs"] = _build_vecs(inp, b)
        m["biasblk"] = bb
        in_maps.append(m)
    res = run_bass_kernel_spmd(nc, in_maps, core_ids=list(range(NB)))
    out = np.empty((NB, SEQ, D), np.float32)
    for b in range(NB):
        out[b] = res.results[b]["yT"].T
    return out
```

```python
import numpy as np
from contextlib import ExitStack

import concourse.bass as bass
import concourse.mybir as mybir
from concourse.bass_utils import run_bass_kernel_spmd

F32 = mybir.dt.float32
BF16 = mybir.dt.bfloat16
AF = mybir.ActivationFunctionType
ALU = mybir.AluOpType

D = 1024
P = 128
KD = 8
SEQ = 8192
NB = 8
FFN = 2816
NFC = FFN // P
A_IN = 4096
EPS = 1e-6
T = 256
NCH = T // 64
NPAIR = T // 128
NEG = -30000.0


class _Op:
    __slots__ = ("eng", "fn", "deps", "dma", "need", "sig", "idx")

    def __init__(self, eng, fn, deps, dma):
        self.eng = eng
        self.fn = fn
        self.deps = deps
        self.dma = dma
        self.need = False
        self.sig = None
        self.idx = -1


class Prog:
    ENGS = ("pe", "act", "dve", "pool", "sp")
    NDS = 32
    EPOCH = 50000

    def __init__(self, nc):
        self.nc = nc
        self.streams = {e: [] for e in self.ENGS}
        self.lastw = {}
        self.readers = {}
        self.pending = {e: [] for e in self.ENGS}
        self.dmas = []
        self.dmas_since_barrier = []
        self.nops = 0
        self.limit = None

    def op(self, eng, fn, r=(), w=(), dma=False):
        deps = []
        for k in r:
            lw = self.lastw.get(k)
            if lw is not None:
                deps.append(lw)
        for k in w:
            lw = self.lastw.get(k)
            if lw is not None:
                deps.append(lw)
            deps.extend(self.readers.get(k, ()))
        if self.pending[eng]:
            deps.extend(self.pending[eng])
            self.pending[eng] = []
        o = _Op(eng, fn, deps, dma)
        self.nops += 1
        if self.limit is not None and self.nops > self.limit:
            return o
        o.idx = len(self.streams[eng])
        self.streams[eng].append(o)
        for k in r:
            self.readers.setdefault(k, []).append(o)
        for k in w:
            self.lastw[k] = o
            self.readers[k] = []
        if dma:
            self.dmas.append(o)
            self.dmas_since_barrier.append(o)
        return o

    def barrier(self):
        b = [s[-1] for s in self.streams.values() if s]
        b.extend(self.dmas_since_barrier)
        self.dmas_since_barrier = []
        for e in self.ENGS:
            self.pending[e] = list(b)

    @staticmethod
    def _counts(dep, o):
        if dep is o:
            return False
        if dep.eng == "pe" and o.eng == "pe" and not dep.dma and not o.dma:
            return False
        return True

    def emit(self):
        nc = self.nc
        for s in self.streams.values():
            for o in s:
                for d in o.deps:
                    if self._counts(d, o):
                        d.need = True
        with ExitStack() as es:
            esems = {}
            for e in self.ENGS:
                n = sum(1 for o in self.streams[e] if o.need and not o.dma)
                ne = max(1, -(-n // self.EPOCH))
                esems[e] = [es.enter_context(nc.semaphore(f"s_{e}_{i}")) for i in range(ne)]
                cnt = 0
                for o in self.streams[e]:
                    if o.need and not o.dma:
                        o.sig = (esems[e][cnt // self.EPOCH], cnt % self.EPOCH + 1)
                        cnt += 1
            by_eng = {}
            for o in self.dmas:
                by_eng.setdefault(o.eng, []).append(o)
            for en, lst in by_eng.items():
                nds = min(self.NDS, len(lst))
                dsems = [es.enter_context(nc.semaphore(f"s_dma_{en}_{i}")) for i in range(nds)]
                for i, o in enumerate(lst):
                    o.sig = (dsems[i % nds], 16 * (i // nds + 1))
                    if i >= nds:
                        o.deps.append(lst[i - nds])
            final_waits = {}
            for o in self.dmas:
                final_waits[id(o.sig[0])] = o.sig
            engmap = {"pe": "tensor", "act": "scalar", "dve": "vector", "pool": "gpsimd", "sp": "sync"}
            block = es.enter_context(nc.Block())

            def make(ename):
                def body(e):
                    waited = {}
                    for o in self.streams[ename]:
                        for d in o.deps:
                            if not self._counts(d, o):
                                continue
                            sem, val = d.sig
                            if waited.get(id(sem), 0) >= val:
                                continue
                            e.wait_ge(sem, val)
                            waited[id(sem)] = val
                        ins = o.fn(e)
                        if o.dma:
                            ins.then_inc(o.sig[0], 16)
                        elif o.need:
                            ins.then_inc(o.sig[0], 1)
                    if ename == "sp":
                        for sem, val in final_waits.values():
                            if waited.get(id(sem), 0) < val:
                                e.wait_ge(sem, val)
                return body

            for ename in self.ENGS:
                getattr(block, engmap[ename])(make(ename))


def _pk(v):
    v = np.asarray(v, np.float32)
    return np.ascontiguousarray(v.reshape(-1, P).T)


class VecLayout:
    def __init__(self):
        self.off = {}
        self.n = 0

    def add(self, name, ncols):
        self.off[name] = (self.n, ncols)
        self.n += ncols


def _vec_layout():
    vl = VecLayout()
    vl.add("c", 8)
    for l in range(4):
        vl.add(f"modb{l}", 48)
        vl.add(f"nmix{l}", 8)
        vl.add(f"nffn{l}", 8)
    vl.add("alb0", 8)
    vl.add("alb1", 8)
    vl.add("aon0", 1)
    vl.add("aon1", 1)
    vl.add("kvn", 8)
    vl.add("kvmodb", 16)
    vl.add("fn", 8)
    vl.add("ident", 128)
    vl.add("tri", 64)
    vl.add("scanmask", T)
    return vl


VL = _vec_layout()


def _build_vecs(inp, b):
    v = np.zeros((P, VL.n), np.float32)

    def put(name, arr):
        o, n = VL.off[name]
        assert arr.shape == (P, n), (name, arr.shape, n)
        v[:, o:o + n] = arr

    put("c", _pk(inp["c"][b]))
    for l in range(4):
        put(f"modb{l}", _pk(inp["mod_b"][l]))
        put(f"nmix{l}", _pk(inp["norm_mix"][l]))
        put(f"nffn{l}", _pk(inp["norm_ffn"][l]))
    put("alb0", _pk(inp["a_lb"][0]))
    put("alb1", _pk(inp["a_lb"][1]))
    put("aon0", _pk(inp["a_out_norm"][0]))
    put("aon1", _pk(inp["a_out_norm"][1]))
    put("kvn", _pk(inp["kv_norm"]))
    put("kvmodb", _pk(inp["kv_mod_b"]))
    put("fn", _pk(inp["final_norm"]))
    put("ident", np.eye(P, dtype=np.float32))
    s = np.arange(P)[:, None] % 64
    t = np.arange(64)[None, :]
    put("tri", (s <= t).astype(np.float32))
    sm = np.ones((P, T), np.float32)
    sm[:, ::64] = 0.0
    put("scanmask", sm)
    return v


def _build_bias_blocks(rel_bias):
    rb = np.asarray(rel_bias, np.float32)
    p = np.arange(P)
    half = p // 64
    ki = p % 64
    qi = np.arange(64)
    out = np.full((2, P, 16, 10, 64), NEG, np.float32)
    for d1 in range(10):
        dl = d1 - half
        valid = (dl >= 0) & (dl <= 8)
        rel = np.clip(ki[:, None] - qi[None, :] - 64 * dl[:, None], -256, 63) + 256
        for l in range(2):
            g = rb[l][rel]
            g = np.transpose(g, (0, 2, 1))
            out[l, valid, :, d1, :] = g[valid]
    return np.ascontiguousarray(out.reshape(2, P, 16 * 640))


ALL_PHASES = ("pro", "m0", "f0", "m1", "f1", "m2", "f2", "m3", "f3")


class Builder:
    def __init__(self, S=SEQ, phases=ALL_PHASES, h_in=False, h_out=False):
        self.S = S
        self.NT = S // T
        self.phases = tuple(phases)
        self.h_in = h_in
        self.h_out = h_out
        self.nc = bass.Bass("TRN2", target_bir_lowering=False)
        self.pg = Prog(self.nc)
        self._snap = None
        self._applied = None
        self.pslot = 0
        self.pbuf = []
        self.pbanks = [0, 1]

    def op(self, eng, fn, **k):
        snap = self._snap
        if snap is None:
            return self.pg.op(eng, fn, **k)

        def fn2(e, fn=fn, snap=snap):
            if self._applied is not snap:
                self.__dict__.update(snap)
                self._applied = snap
            return fn(e)
        return self.pg.op(eng, fn2, **k)

    def freeze(self):
        self._snap = {k: v for k, v in self.__dict__.items() if k not in ("_snap", "_applied", "pg", "nc", "es")}

    def dram_in(self, name, shape, dt=F32):
        return self.nc.dram_tensor(name, list(shape), dt, kind="ExternalInput").ap()

    def pipe_add(self, mm, ev, ncols=T):
        self.pbuf.append((mm, ev, ncols))
        if len(self.pbuf) == 2:
            self.pipe_flush()

    def pipe_flush(self):
        if not self.pbuf:
            return
        bank = self.pbanks[self.pslot % len(self.pbanks)]
        self.pslot += 1
        key = ("ps", bank)
        aps = []
        off = 0
        for mm, ev, n in self.pbuf:
            ap = self.psF[:, bank, off:off + n]
            off += n
            aps.append(ap)
            mm(ap, key)
        for (mm, ev, n), ap in zip(self.pbuf, aps):
            ev(ap, key)
        self.pbuf = []

    def reset_alloc(self):
        self.aoff = 0

    def _alloc_words(self, nwords):
        nwords = (nwords + 7) // 8 * 8
        o = self.aoff
        self.aoff += nwords
        assert self.aoff <= self.RW, f"region overflow {self.aoff} > {self.RW}"
        return o

    def f32(self, n):
        o = self._alloc_words(n)
        return self.R[:, o:o + n]

    def bf(self, n):
        o = self._alloc_words((n + 1) // 2)
        return self.R[:, o:o + (n + 1) // 2].bitcast(BF16)[:, 0:n]

    def build(self):
        nc = self.nc
        S = self.S
        with ExitStack() as es:
            self.es = es
            self.xT = self.dram_in("xT", [D, S])
            self.vecs_d = self.dram_in("vecs", [P, VL.n])
            self.mod_w = self.dram_in("mod_w", [4, D, 6144])
            self.kv_mod_w = self.dram_in("kv_mod_w", [D, 2048])
            self.ffn_w_in = self.dram_in("ffn_w_in", [4, D, 2 * FFN])
            self.ffn_w_out = self.dram_in("ffn_w_out", [4, FFN, D])
            self.a_w_in = self.dram_in("a_w_in", [2, D, A_IN])
            self.a_w_out = self.dram_in("a_w_out", [2, D, D])
            self.kv_w = self.dram_in("kv_w", [D, 2048])
            self.b_w_q = self.dram_in("b_w_q", [2, D, D])
            self.b_w_o = self.dram_in("b_w_o", [2, D, D])
            self.biasblk = self.dram_in("biasblk", [2, P, 16 * 640])
            self.yT = nc.dram_tensor("yT", [D, S], F32, kind="ExternalOutput").ap()
            if self.h_in:
                self.H_in = self.dram_in("H_in", [D, S])
                self.KT_in = self.dram_in("KT_in", [D, S], BF16)
                self.VV_in = self.dram_in("VV_in", [S, 2 * D], BF16)
                self.par_in = self.dram_in("par_in", [P, 512])
            if self.h_out:
                self.H = nc.dram_tensor("H", [D, S], F32, kind="ExternalOutput").ap()
                self.KTs = nc.dram_tensor("KTs", [D, S], BF16, kind="ExternalOutput").ap()
                self.VVs = nc.dram_tensor("VVs", [S, 2 * D], BF16, kind="ExternalOutput").ap()
                self.par_out = nc.dram_tensor("par_out", [P, 512], F32, kind="ExternalOutput").ap()
            else:
                self.H = nc.dram_tensor("H", [D, S], F32).ap()
                self.KTs = nc.dram_tensor("KTs", [D, S], BF16).ap()
                self.VVs = nc.dram_tensor("VVs", [S, 2 * D], BF16).ap()

            self.vecs = es.enter_context(nc.sbuf_tensor("vecs_sb", [P, VL.n], F32))
            self.par = es.enter_context(nc.sbuf_tensor("par_sb", [P, 512], F32))
            self.cb = es.enter_context(nc.sbuf_tensor("cb_sb", [P, 512], BF16))
            self.state = es.enter_context(nc.sbuf_tensor("state_sb", [P, 8, 128], F32))
            self.cb2 = es.enter_context(nc.sbuf_tensor("cb2_sb", [P, 8], F32))
            self.RW = 49500
            self.R = es.enter_context(nc.sbuf_tensor("region", [P, self.RW], F32))
            self.psF = es.enter_context(nc.psum_tensor("psF", [P, 7, 512], F32))
            self.psB = es.enter_context(nc.psum_tensor("psB", [P, 1024], BF16))

            self.ident = self.cb[:, 0:128]
            self.ones = self.cb[:, 128:256]
            self.tri = self.cb[:, 256:320]
            self.zeros32 = self.cb[:, 320:352]

            self.freeze()
            self.setup_consts()
            ph = self.phases
            if "pro" in ph:
                self.prologue()
            else:
                self.load_params()
            first_src_is_x = "pro" in ph
            src_is_x = first_src_is_x
            mix_layers = {"m0": 0, "m1": 1, "m2": 2, "m3": 3}
            ffn_layers = {"f0": 0, "f1": 1, "f2": 2, "f3": 3}
            todo = [p for p in ph if p != "pro"]
            for i, p in enumerate(todo):
                self.pg.barrier()
                if src_is_x:
                    src = self.xT
                elif i == 0 and self.h_in:
                    src = self.H_in
                else:
                    src = self.H
                last = (p == "f3")
                dst = self.yT if last else self.H
                if p in ("m0", "m1"):
                    self.hgrn_phase(mix_layers[p], src, dst)
                elif p in ("m2", "m3"):
                    self.att_phase(mix_layers[p], src, dst, first=(i == 0 and self.h_in))
                else:
                    self.ffn_phase(ffn_layers[p], src, dst, final=last)
                src_is_x = False
            if self.h_out:
                self.pg.barrier()
                self.op("sp", lambda e: e.dma_start(out=self.par_out, in_=self.par[:]), r=[("par",)], w=[("par_out",)], dma=True)
            self.pg.emit()
        return nc

    def vcol(self, name):
        o, n = VL.off[name]
        return self.vecs[:, o:o + n]

    def setup_consts(self):
        op = self.op
        op("sp", lambda e: e.dma_start(out=self.vecs[:], in_=self.vecs_d), w=[("vecs",)], dma=True)
        op("dve", lambda e: e.tensor_copy(out=self.ident, in_=self.vcol("ident")), r=[("vecs",)], w=[("cb",)])
        op("dve", lambda e: e.memset(self.ones, 1.0), w=[("cb",)])
        op("dve", lambda e: e.memset(self.zeros32, 0.0), w=[("cb",)])
        if "pro" in self.phases:
            op("dve", lambda e: e.memset(self.par[:], 0.0), w=[("par",)])
        op("dve", lambda e: e.tensor_copy(out=self.tri, in_=self.vcol("tri")), r=[("vecs",)], w=[("cb",)])
        op("dve", lambda e: e.memset(self.state[:], 0.0), w=[("S", h) for h in range(8)])
        self.scanmask = self.vcol("scanmask")
        self.trif = self.vcol("tri")
        op("dve", lambda e: e.memset(self.cb2[:, 0:1], EPS), w=[("cb2",)])
        self.epsc = self.cb2[:, 0:1]

    def pcol(self, l, name):
        o = {"a1": 0, "b1": 8, "g1": 16, "a2": 24, "b2": 32, "g2": 40}[name]
        return self.par[:, 64 * l + o: 64 * l + o + 8]

    def hcol(self, l, name):
        o = {"s_a": 0, "ns_a": 8, "b_a": 16}[name]
        return self.par[:, 272 + 32 * l + o: 272 + 32 * l + o + 8]

    def load_params(self):
        self.op("sp", lambda e: e.dma_start(out=self.par[:], in_=self.par_in), w=[("par",)], dma=True)

    def prologue(self):
        op = self.op
        self.reset_alloc()
        self.freeze()
        cact = self.f32(8)
        op("act", lambda e: e.activation(out=cact, in_=self.vcol("c"), func=AF.Silu), r=[("vecs",)], w=[("cact",)])
        NPIECE = 1024
        wb = [self.f32(8 * NPIECE).rearrange("p (k n) -> p k n", k=8) for _ in range(2)]
        modsb = self.f32(64)
        piece = 0
        jobs = [(self.mod_w[l], 6144, l) for l in range(4)] + [(self.kv_mod_w, 2048, 4)]
        for wd, ncol, l in jobs:
            nm = ncol // P
            psm = self.psF[:, 0, 0:nm]
            for pc in range(ncol // NPIECE):
                buf = wb[piece % 2]
                bk = ("modw", piece % 2)
                piece += 1
                src = wd.rearrange("(k p) n -> p k n", p=P)[:, :, pc * NPIECE:(pc + 1) * NPIECE]
                op("sp", lambda e, buf=buf, src=src: e.dma_start(out=buf, in_=src), w=[bk], dma=True)
                for mm in range(NPIECE // P):
                    m = pc * (NPIECE // P) + mm
                    for k in range(8):
                        op("pe", lambda e, buf=buf, mm=mm, k=k, m=m, psm=psm: e.matmul(
                            psm[:, m:m + 1], lhsT=buf[:, k, mm * P:(mm + 1) * P], rhs=cact[:, k:k + 1],
                            start=(k == 0), stop=(k == 7)), r=[bk, ("cact",)], w=[("ps", 0)])
            if l < 4:
                op("dve", lambda e, psm=psm, l=l: e.tensor_tensor(out=modsb[:, 0:48], in0=psm, in1=self.vcol(f"modb{l}"), op=ALU.add),
                   r=[("ps", 0), ("vecs",)], w=[("modsb",)])
                for (nm_, ncolname, so, go, sh) in (("nmix", "a1", 8, 16, 0), ("nffn", "a2", 32, 40, 24)):
                    a = self.pcol(l, ncolname)
                    bcol = self.pcol(l, "b1" if ncolname == "a1" else "b2")
                    gcol = self.pcol(l, "g1" if ncolname == "a1" else "g2")
                    op("dve", lambda e, a=a, so=so, l=l, nm_=nm_: e.scalar_tensor_tensor(
                        out=a, in0=modsb[:, so:so + 8], scalar=1.0, in1=self.vcol(f"{nm_}{l}"), op0=ALU.add, op1=ALU.mult),
                       r=[("modsb",), ("vecs",)], w=[("par",)])
                    op("dve", lambda e, bcol=bcol, sh=sh: e.tensor_copy(out=bcol, in_=modsb[:, sh:sh + 8]), r=[("modsb",)], w=[("par",)])
                    op("dve", lambda e, gcol=gcol, go=go: e.tensor_copy(out=gcol, in_=modsb[:, go:go + 8]), r=[("modsb",)], w=[("par",)])
            else:
                op("dve", lambda e, psm=psm: e.tensor_tensor(out=modsb[:, 0:16], in0=psm, in1=self.vcol("kvmodb"), op=ALU.add),
                   r=[("ps", 0), ("vecs",)], w=[("modsb",)])
                op("dve", lambda e: e.scalar_tensor_tensor(out=self.par[:, 256:264], in0=modsb[:, 8:16], scalar=1.0, in1=self.vcol("kvn"),
                                                           op0=ALU.add, op1=ALU.mult), r=[("modsb",), ("vecs",)], w=[("par",)])
                op("dve", lambda e: e.tensor_copy(out=self.par[:, 264:272], in_=modsb[:, 0:8]), r=[("modsb",)], w=[("par",)])
        a0 = self.vcol("alb0")
        a1 = self.vcol("alb1")
        t = [self.f32(8) for _ in range(8)]
        mx, e0, e1, den, sm0, sm1, cs1, lbt = t
        seq = [
            ("dve", lambda e: e.tensor_tensor(out=mx, in0=a0, in1=a1, op=ALU.max)),
            ("dve", lambda e: e.tensor_tensor(out=e0, in0=a0, in1=mx, op=ALU.subtract)),
            ("dve", lambda e: e.tensor_tensor(out=e1, in0=a1, in1=mx, op=ALU.subtract)),
            ("act", lambda e: e.activation(out=e0, in_=e0, func=AF.Exp)),
            ("act", lambda e: e.activation(out=e1, in_=e1, func=AF.Exp)),
            ("dve", lambda e: e.tensor_tensor(out=den, in0=e0, in1=e1, op=ALU.add)),
            ("dve", lambda e: e.reciprocal(out=den, in_=den)),
            ("dve", lambda e: e.tensor_tensor(out=sm0, in0=e0, in1=den, op=ALU.mult)),
            ("dve", lambda e: e.tensor_tensor(out=sm1, in0=e1, in1=den, op=ALU.mult)),
            ("dve", lambda e: e.tensor_tensor(out=cs1, in0=sm0, in1=sm1, op=ALU.add)),
        ]
        for eng, fn in seq:
            op(eng, fn, r=[("vecs",), ("lbtmp",)], w=[("lbtmp",)])
        for l in range(2):
            cs = sm0 if l == 0 else cs1
            op("dve", lambda e, cs=cs: e.tensor_tensor(out=lbt, in0=cs, in1=sm0, op=ALU.subtract), r=[("lbtmp",)], w=[("lbtmp",)])
            op("dve", lambda e, l=l: e.tensor_scalar(out=self.hcol(l, "s_a"), in0=lbt, scalar1=-0.5, scalar2=0.5, op0=ALU.mult, op1=ALU.add),
               r=[("lbtmp",)], w=[("par",)])
            op("dve", lambda e, l=l: e.tensor_scalar(out=self.hcol(l, "ns_a"), in0=lbt, scalar1=0.5, scalar2=-0.5, op0=ALU.mult, op1=ALU.add),
               r=[("lbtmp",)], w=[("par",)])
            op("dve", lambda e, l=l: e.tensor_scalar(out=self.hcol(l, "b_a"), in0=lbt, scalar1=0.5, scalar2=0.5, op0=ALU.mult, op1=ALU.add),
               r=[("lbtmp",)], w=[("par",)])

    def tile_src(self, src, ti):
        return src.rearrange("(k p) t -> p k t", p=P)[:, :, ti * T:(ti + 1) * T]

    def load_h(self, src, ti, srckey):
        par = ti % 2
        h = self.h[par]
        self.op("sp", lambda e: e.dma_start(out=h, in_=self.tile_src(src, ti)),
                r=[(srckey, ti)], w=[("h", par, k) for k in range(8)], dma=True)

    def store_h(self, dst, ti, dstkey):
        par = ti % 2
        h = self.h[par]
        self.op("sp", lambda e: e.dma_start(out=self.tile_src(dst, ti), in_=h),
                r=[("h", par, k) for k in range(8)], w=[(dstkey, ti)], dma=True)

    def rstd(self, h, hk):
        op = self.op
        u = self.u
        uk = [("u", k) for k in range(8)]
        op("act", lambda e: e.activation(out=u.rearrange("p k t -> p (k t)"), in_=h.rearrange("p k t -> p (k t)"), func=AF.Square),
           r=hk, w=uk)

        def mm(ps, pk):
            for k in range(8):
                op("pe", lambda e, k=k: e.matmul(ps, lhsT=self.ones, rhs=u[:, k, :], start=(k == 0), stop=(k == 7)),
                   r=[uk[k], ("cb",)], w=[pk])

        def ev(ps, pk):
            op("act", lambda e: e.activation(out=self.lnt, in_=ps, func=AF.Ln, scale=1.0 / D, bias=self.epsc), r=[pk, ("cb2",)], w=[("lnt",)])
            op("act", lambda e: e.activation(out=self.rs, in_=self.lnt, func=AF.Exp, scale=-0.5), r=[("lnt",)], w=[("rs",)])

        self.pipe_flush()
        self.pipe_add(mm, ev)
        self.pipe_flush()

    def affine(self, h, hk, a, b, u, uname):
        op = self.op
        for k in range(8):
            tmp = self.tmp[k % 2]
            tk = ("tmp", k % 2)
            op("dve", lambda e, k=k, tmp=tmp: e.scalar_tensor_tensor(out=tmp, in0=h[:, k, :], scalar=a[:, k:k + 1], in1=self.rs,
                                                                     op0=ALU.mult, op1=ALU.mult),
               r=[hk[k], ("rs",), ("par",)], w=[tk])
            op("act", lambda e, k=k, tmp=tmp: e.activation(out=u[:, k, :], in_=tmp, func=AF.Identity, bias=b[:, k:k + 1], scale=1.0),
               r=[tk, ("par",)], w=[(uname, k)])

    def load_w(self, dst3, src2, nk, key):
        for k in range(nk):
            self.op("pool", lambda e, k=k: e.dma_start(out=dst3[:, k, :], in_=src2[k * P:(k + 1) * P, :]), w=[(key, k)], dma=True)

    def mm_fm(self, wt, wkey, col0, u, uname):
        def mm(ps, pk):
            for k in range(8):
                self.op("pe", lambda e, k=k: e.matmul(ps, lhsT=wt[:, k, col0:col0 + P], rhs=u[:, k, :], start=(k == 0), stop=(k == 7)),
                        r=[(wkey, k), (uname, k)], w=[pk])
        return mm

    def mm_tm(self, wt, wkey, col0, ncol, u, uname, tb):
        def mm(ps, pk):
            for k in range(8):
                self.op("pe", lambda e, k=k: e.matmul(ps, lhsT=u[:, k, tb * P:(tb + 1) * P], rhs=wt[:, k, col0:col0 + ncol],
                                                      start=(k == 0), stop=(k == 7)),
                        r=[(wkey, k), (uname, k)], w=[pk])
        return mm

    def ev_act(self, out, okey, func, scale=1.0):
        def ev(ps, pk):
            self.op("act", lambda e: e.activation(out=out, in_=ps, func=func, scale=scale), r=[pk], w=[okey])
        return ev

    def ev_dve_copy(self, out, okey):
        def ev(ps, pk):
            self.op("dve", lambda e: e.tensor_copy(out=out, in_=ps), r=[pk], w=[okey])
        return ev

    def out_proj_residual(self, wt, wkey, nk, x, xname, h, hk, g):
        op = self.op
        for m in range(8):
            def mm(ps, pk, m=m):
                for k in range(nk):
                    op("pe", lambda e, k=k: e.matmul(ps, lhsT=wt[:, k, m * P:(m + 1) * P], rhs=x[:, k, :],
                                                     start=(k == 0), stop=(k == nk - 1)),
                       r=[(wkey, k), (xname, k)], w=[pk])

            def ev(ps, pk, m=m):
                op("dve", lambda e: e.scalar_tensor_tensor(out=h[:, m, :], in0=ps, scalar=g[:, m:m + 1], in1=h[:, m, :],
                                                           op0=ALU.mult, op1=ALU.add),
                   r=[pk, hk[m], ("par",)], w=[hk[m]])
            self.pipe_add(mm, ev)
        self.pipe_flush()

    def hgrn_phase(self, L, src, dst):
        op = self.op
        self.reset_alloc()
        self.pbanks = [0, 1]
        self.win = self.bf(8 * A_IN).rearrange("p (k n) -> p k n", k=8)
        self.wout = self.bf(8 * D).rearrange("p (k n) -> p k n", k=8)
        self.u = self.bf(8 * T).rearrange("p (k t) -> p k t", k=8)
        self.q = self.bf(8 * T).rearrange("p (k t) -> p k t", k=8)
        self.g = self.bf(8 * T).rearrange("p (k t) -> p k t", k=8)
        self.on = self.bf(8 * T).rearrange("p (k t) -> p k t", k=8)
        self.v = self.bf(NPAIR * D).rearrange("p (b n) -> p b n", b=NPAIR)
        self.h = [self.f32(8 * T).rearrange("p (k t) -> p k t", k=8) for _ in range(2)]
        self.th = self.f32(8 * T).rearrange("p (k t) -> p k t", k=8)
        self.lnt = self.f32(T)
        self.rs = self.f32(T)
        self.tmp = [self.f32(T) for _ in range(2)]
        R2 = range(4)
        self.Ep = [self.bf(T) for _ in R2]
        self.Em = [self.bf(T) for _ in R2]
        self.qt = [self.bf(T) for _ in R2]
        self.kt = [self.bf(T) for _ in R2]
        self.ktT = [self.bf(NCH * P) for _ in R2]
        self.at = [self.bf(NCH * 64) for _ in R2]
        self.osq = [self.bf(T) for _ in R2]
        self.Ab = [self.bf(128) for _ in range(4)]
        self.X = [self.f32(NCH * 128) for _ in R2]
        self.lf = [self.f32(T) for _ in R2]
        self.bb = [self.f32(T) for _ in R2]
        self.bm = [self.f32(T) for _ in R2]
        self.kk = [self.f32(T) for _ in R2]
        self.lo = [self.f32(T) for _ in R2]
        self.rso = [self.f32(T) for _ in R2]
        self.t2 = [self.f32(T) for _ in R2]
        self.esm = [self.f32(3 * NCH) for _ in R2]
        self.esm2 = [self.f32(4 * NCH) for _ in R2]
        self.qta = [self.bf(T) for _ in R2]
        self.kta = [self.bf(T) for _ in R2]
        self.actr = 0
        self.freeze()

        self.load_w(self.win, self.a_w_in[L], 8, "win")
        self.load_w(self.wout, self.a_w_out[L], 8, "wout")
        for r_ in R2:
            op("pool", lambda e, r_=r_: e.memset(self.ktT[r_], 0.0), w=[("ktT", r_)])
            op("pool", lambda e, r_=r_: e.memset(self.at[r_], 0.0), w=[("at", r_)])
        op("dve", lambda e: e.memset(self.state[:], 0.0), r=[("S", h) for h in range(8)], w=[("S", h) for h in range(8)])
        srckey = "xT" if src is self.xT else ("Hin" if (self.h_in and src is self.H_in) else "H")
        dstkey = "H"
        self.load_h(src, 0, srckey)
        self.pre_tile(0, self.pcol(L, "a1"), self.pcol(L, "b1"))
        for ti in range(self.NT):
            if ti + 1 < self.NT:
                self.load_h(src, ti + 1, srckey)
            self.hgrn_tile(L, ti)
            if ti + 1 < self.NT:
                self.pre_tile(ti + 1, self.pcol(L, "a1"), self.pcol(L, "b1"))
            par = ti % 2
            self.out_proj_residual(self.wout, "wout", 8, self.on, "on", self.h[par], [("h", par, k) for k in range(8)], self.pcol(L, "g1"))
            self.store_h(dst, ti, dstkey)

    def pre_tile(self, ti, a, b, akv=None, bkv=None):
        par = ti % 2
        h = self.h[par]
        hk = [("h", par, k) for k in range(8)]
        self.rstd(h, hk)
        if akv is not None:
            self.affine(h, hk, akv, bkv, self.ukv, "ukv")
        self.affine(h, hk, a, b, self.u, "u")

    def hgrn_tile(self, L, ti):
        op = self.op
        par = ti % 2
        h = self.h[par]
        hk = [("h", par, k) for k in range(8)]
        u = self.u
        win = self.win
        s_a = self.hcol(L, "s_a")
        ns_a = self.hcol(L, "ns_a")
        b_a = self.hcol(L, "b_a")
        ogain = self.vcol(f"aon{L}")
        for hd in range(8):
            self.pipe_add(self.mm_fm(win, "win", hd * P, u, "u"), self.ev_act(self.q[:, hd, :], ("q", hd), AF.Silu))
            self.pipe_add(self.mm_fm(win, "win", 1024 + hd * P, u, "u"), self.ev_act(self.th[:, hd, :], ("th", hd), AF.Tanh, 0.5))
        for hd in range(8):
            self.pipe_add(self.mm_fm(win, "win", 3072 + hd * P, u, "u"), self.ev_act(self.g[:, hd, :], ("g", hd), AF.Silu))
        for tb in range(NPAIR):
            for nq in range(4):
                self.pipe_add(self.mm_tm(win, "win", 2048 + nq * 256, 256, u, "u", tb),
                              self.ev_dve_copy(self.v[:, tb, nq * 256:(nq + 1) * 256], ("v", tb, nq)))
        self.pipe_flush()
        def head_gen(hd):
            r = hd % 4
            lf, bb, bm, kk = self.lf[r], self.bb[r], self.bm[r], self.kk[r]
            Ep, Em, qt, kt, ktT, at = self.Ep[r], self.Em[r], self.qt[r], self.kt[r], self.ktT[r], self.at[r]
            esm = self.esm[r]
            X = self.X[r]
            emid, elast, elm = esm[:, 0:NCH], esm[:, NCH:2 * NCH], esm[:, 2 * NCH:3 * NCH]
            th = self.th[:, hd, :]
            op("act", lambda e, hd=hd, lf=lf, th=th: e.activation(out=lf, in_=th, func=AF.Ln, scale=s_a[:, hd:hd + 1], bias=b_a[:, hd:hd + 1]),
               r=[("th", hd), ("par",)], w=[("lf", r)])
            op("dve", lambda e, hd=hd, kk=kk, th=th: e.tensor_scalar(out=kk, in0=th, scalar1=ns_a[:, hd:hd + 1], scalar2=s_a[:, hd:hd + 1],
                                                                     op0=ALU.mult, op1=ALU.add),
               r=[("th", hd), ("par",)], w=[("kk", r)])
            op("dve", lambda e, lf=lf, bb=bb: e.tensor_tensor_scan(out=bb, data0=self.scanmask, data1=lf, initial=0.0, op0=ALU.mult, op1=ALU.add),
               r=[("lf", r), ("vecs",)], w=[("bb", r)])
            bb3 = bb.rearrange("p (c t) -> p c t", t=64)
            bm3 = bm.rearrange("p (c t) -> p c t", t=64)
            op("dve", lambda e, bb3=bb3, bm3=bm3: e.tensor_tensor(out=bm3, in0=bb3, in1=bb3[:, :, 31:32].to_broadcast([P, NCH, 64]), op=ALU.subtract),
               r=[("bb", r)], w=[("bm", r)])
            yield
            op("act", lambda e, Ep=Ep, bm=bm: e.activation(out=Ep, in_=bm, func=AF.Exp), r=[("bm", r)], w=[("Ep", r)])
            op("act", lambda e, Em=Em, bm=bm: e.activation(out=Em, in_=bm, func=AF.Exp, scale=-1.0), r=[("bm", r)], w=[("Em", r)])
            op("act", lambda e, emid=emid, bb3=bb3: e.activation(out=emid, in_=bb3[:, :, 31], func=AF.Exp), r=[("bb", r)], w=[("esm", r)])
            op("act", lambda e, elast=elast, bb3=bb3: e.activation(out=elast, in_=bb3[:, :, 63], func=AF.Exp), r=[("bb", r)], w=[("esm", r)])
            op("act", lambda e, elm=elm, bm3=bm3: e.activation(out=elm, in_=bm3[:, :, 63], func=AF.Exp), r=[("bm", r)], w=[("esm", r)])
            yield
            op("dve", lambda e, hd=hd, qt=qt, Ep=Ep: e.tensor_tensor(out=qt, in0=self.q[:, hd, :], in1=Ep, op=ALU.mult),
               r=[("q", hd), ("Ep", r)], w=[("qt", r)])
            op("dve", lambda e, kt=kt, kk=kk, Em=Em: e.tensor_tensor(out=kt, in0=kk, in1=Em, op=ALU.mult),
               r=[("kk", r), ("Em", r)], w=[("kt", r)])
            yield
            pst = self.psB[:, 0:T]
            for p_ in range(NPAIR):
                op("pe", lambda e, p_=p_, kt=kt: e.transpose(out=pst[:, p_ * P:(p_ + 1) * P], in_=kt[:, p_ * P:(p_ + 1) * P], identity=self.ident),
                   r=[("kt", r), ("cb",)], w=[("psB",)])
            for hf in range(2):
                op("dve", lambda e, ktT=ktT, hf=hf: e.tensor_copy(
                    out=ktT.rearrange("p (a b d) -> p a b d", b=2, d=P)[hf * 64:(hf + 1) * 64, :, hf, :],
                    in_=pst.rearrange("p (a d) -> p a d", d=P)[hf * 64:(hf + 1) * 64, :, :]), r=[("psB",)], w=[("ktT", r)])
            bmid4 = bm.rearrange("p (j t) -> p j t", t=32)[:, :, 15]
            sa, sb = self.esm2[r][:, 0:2 * NCH], self.esm2[r][:, 2 * NCH:4 * NCH]
            op("act", lambda e, sa=sa, bmid4=bmid4: e.activation(out=sa, in_=bmid4, func=AF.Exp, scale=-1.0), r=[("bm", r)], w=[("esm2", r)])
            op("act", lambda e, sb=sb, bmid4=bmid4: e.activation(out=sb, in_=bmid4, func=AF.Exp), r=[("bm", r)], w=[("esm2", r)])
            qta, kta = self.qta[r], self.kta[r]
            op("dve", lambda e, qta=qta, qt=qt, sa=sa: e.tensor_tensor(
                out=qta.rearrange("p (j t) -> p j t", t=32), in0=qt.rearrange("p (j t) -> p j t", t=32),
                in1=sa.rearrange("p (j o) -> p j o", o=1).to_broadcast([P, 2 * NCH, 32]), op=ALU.mult),
               r=[("qt", r), ("esm2", r)], w=[("qta", r)])
            op("dve", lambda e, kta=kta, kt=kt, sb=sb: e.tensor_tensor(
                out=kta.rearrange("p (j t) -> p j t", t=32), in0=kt.rearrange("p (j t) -> p j t", t=32),
                in1=sb.rearrange("p (j o) -> p j o", o=1).to_broadcast([P, 2 * NCH, 32]), op=ALU.mult),
               r=[("kt", r), ("esm2", r)], w=[("kta", r)])
            psat = self.psF[:, 2, 0:NPAIR * 64]
            for c in range(NCH):
                p_, hf = c // 2, c % 2
                rb, cb_, t0 = hf * 64, p_ * 64, c * 64
                quads = [
                    (rb, cb_, kta[:, t0:t0 + 32], qta[:, t0:t0 + 32]),
                    (rb, cb_ + 32, kt[:, t0:t0 + 32], qt[:, t0 + 32:t0 + 64]),
                    (rb + 32, cb_ + 32, kta[:, t0 + 32:t0 + 64], qta[:, t0 + 32:t0 + 64]),
                    (rb + 32, cb_, self.zeros32, qt[:, t0:t0 + 32]),
                ]
                for (r0, c0_, l_, r_) in quads:
                    op("pe", lambda e, r0=r0, c0_=c0_, l_=l_, r_=r_: e.matmul(
                        psat[r0:r0 + 32, c0_:c0_ + 32], lhsT=l_, rhs=r_, start=True, stop=True, tile_position=(0, r0)),
                       r=[("kt", r), ("qt", r), ("kta", r), ("qta", r), ("cb",)], w=[("ps", 2)])
            for hf in range(2):
                op("dve", lambda e, at=at, hf=hf: e.tensor_tensor(
                    out=at.rearrange("p (a b t) -> p a b t", b=2, t=64)[hf * 64:(hf + 1) * 64, :, hf, :],
                    in0=psat.rearrange("p (a t) -> p a t", t=64)[hf * 64:(hf + 1) * 64, :, :],
                    in1=self.trif.rearrange("p (a t) -> p a t", a=1)[hf * 64:(hf + 1) * 64, :, :].to_broadcast([64, NPAIR, 64]), op=ALU.mult),
                   r=[("ps", 2), ("vecs",)], w=[("at", r)])
            dbank = 3 + hd % 2
            vsls = []
            for c in range(NCH):
                p_, hf = c // 2, c % 2
                vsl = self.v[:, p_, hd * P:(hd + 1) * P]
                vsls.append(vsl)
                vkeys = [("v", p_, (hd * P) // 256)]
                psd = self.psF[:, dbank, c * P:(c + 1) * P]
                op("pe", lambda e, c=c, psd=psd, ktT=ktT, vsl=vsl: e.matmul(
                    psd, lhsT=ktT[:, c * P:(c + 1) * P], rhs=vsl, start=True, stop=True),
                   r=vkeys + [("ktT", r)], w=[("ps", dbank)])
            for c in range(NCH):
                psd = self.psF[:, dbank, c * P:(c + 1) * P]
                op("act", lambda e, c=c, psd=psd, X=X, elm=elm: e.activation(out=X[:, c * P:(c + 1) * P], in_=psd, func=AF.Copy, scale=elm[:, c:c + 1]),
                   r=[("ps", dbank), ("esm", r)], w=[("X", r, c)])
            yield
            obank = 5 + hd % 2
            pso = self.psF[:, obank, 0:T]
            S_hd = self.state[:, hd, :]
            for c in range(NCH):
                p_, hf = c // 2, c % 2
                ar = self.actr % 4
                self.actr += 1
                Ab = self.Ab[ar]
                vsl = vsls[c]
                vkeys = [("v", p_, (hd * P) // 256)]
                op("act", lambda e, c=c, Ab=Ab, S_hd=S_hd, emid=emid: e.activation(out=Ab, in_=S_hd, func=AF.Copy, scale=emid[:, c:c + 1]),
                   r=[("S", hd), ("esm", r)], w=[("Ab", ar)])
                op("pe", lambda e, c=c, vsl=vsl, at=at: e.matmul(
                    pso[:, c * 64:(c + 1) * 64], lhsT=vsl, rhs=at[:, c * 64:(c + 1) * 64], start=True, stop=False),
                   r=vkeys + [("at", r)], w=[("ps", obank)])
                op("pe", lambda e, c=c, Ab=Ab, qt=qt: e.matmul(
                    pso[:, c * 64:(c + 1) * 64], lhsT=Ab, rhs=qt[:, c * 64:(c + 1) * 64], start=False, stop=True),
                   r=[("Ab", ar), ("qt", r)], w=[("ps", obank)])
                op("dve", lambda e, c=c, S_hd=S_hd, X=X, elast=elast: e.scalar_tensor_tensor(
                    out=S_hd, in0=S_hd, scalar=elast[:, c:c + 1], in1=X[:, c * P:(c + 1) * P], op0=ALU.mult, op1=ALU.add),
                   r=[("X", r, c), ("S", hd), ("esm", r)], w=[("S", hd)])
                if c + 1 < NCH:
                    yield
            yield
            osq, lo, rso, t2 = self.osq[r], self.lo[r], self.rso[r], self.t2[r]
            op("act", lambda e, osq=osq: e.activation(out=osq, in_=pso, func=AF.Square), r=[("ps", obank)], w=[("osq", r)])

            def mm(ps, pk, osq=osq):
                op("pe", lambda e: e.matmul(ps, lhsT=self.ones, rhs=osq, start=True, stop=True), r=[("osq", r), ("cb",)], w=[pk])

            def ev(ps, pk, lo=lo, rso=rso, r=r):
                op("act", lambda e: e.activation(out=lo, in_=ps, func=AF.Ln, scale=1.0 / 128.0, bias=self.epsc), r=[pk, ("cb2",)], w=[("lo", r)])
                op("act", lambda e: e.activation(out=rso, in_=lo, func=AF.Exp, scale=-0.5), r=[("lo", r)], w=[("rso", r)])
            self.pipe_add(mm, ev)
            self.pipe_flush()
            op("dve", lambda e, t2=t2, rso=rso: e.scalar_tensor_tensor(out=t2, in0=pso, scalar=ogain[:, 0:1], in1=rso, op0=ALU.mult, op1=ALU.mult),
               r=[("ps", obank), ("rso", r), ("vecs",)], w=[("t2", r)])
            op("pool", lambda e, hd=hd, t2=t2: e.tensor_tensor(out=self.on[:, hd, :], in0=t2, in1=self.g[:, hd, :], op=ALU.mult),
               r=[("t2", r), ("g", hd)], w=[("on", hd)])
        active = []
        next_hd = 0
        tick = 0
        while active or next_hd < 8:
            if next_hd < 8 and tick % 2 == 0 and len(active) < 4:
                active.append(head_gen(next_hd))
                next_hd += 1
            for g_ in list(active):
                try:
                    next(g_)
                except StopIteration:
                    active.remove(g_)
            tick += 1

    def ffn_phase(self, L, src, dst, final):
        op = self.op
        self.reset_alloc()
        self.pbanks = [0, 1, 2, 3, 4, 5, 6]
        self.fwin = self.bf(8 * 2 * FFN).rearrange("p (k n) -> p k n", k=8)
        self.fwout = self.bf(NFC * D).rearrange("p (k n) -> p k n", k=NFC)
        self.u = self.bf(8 * T).rearrange("p (k t) -> p k t", k=8)
        self.actb = self.bf(NFC * T).rearrange("p (k t) -> p k t", k=NFC)
        self.sg = [self.bf(T) for _ in range(4)]
        self.h = [self.f32(8 * T).rearrange("p (k t) -> p k t", k=8) for _ in range(2)]
        self.lnt = self.f32(T)
        self.rs = self.f32(T)
        self.tmp = [self.f32(T) for _ in range(2)]
        self.freeze()
        self.load_w(self.fwin, self.ffn_w_in[L], 8, "fwin")
        self.load_w(self.fwout, self.ffn_w_out[L], NFC, "fwout")
        srckey = "Hin" if (self.h_in and src is self.H_in) else "H"
        dstkey = "yT" if final else "H"
        self.load_h(src, 0, srckey)
        self.pre_tile(0, self.pcol(L, "a2"), self.pcol(L, "b2"))
        for ti in range(self.NT):
            if ti + 1 < self.NT:
                self.load_h(src, ti + 1, srckey)
            self.ffn_tile(L, ti, final)
            if ti + 1 < self.NT and not final:
                self.pre_tile(ti + 1, self.pcol(L, "a2"), self.pcol(L, "b2"))
            self.ffn_tail(L, ti, final)
            if ti + 1 < self.NT and final:
                self.pre_tile(ti + 1, self.pcol(L, "a2"), self.pcol(L, "b2"))
            self.store_h(dst, ti, dstkey)

    def ffn_tile(self, L, ti, final):
        op = self.op
        par = ti % 2
        h = self.h[par]
        hk = [("h", par, k) for k in range(8)]
        u = self.u
        for m in range(NFC):
            sg = self.sg[m % 4]
            sk = ("sg", m % 4)
            def ev_g(ps, pk, sg=sg, sk=sk):
                op("act", lambda e: e.activation(out=sg, in_=ps, func=AF.Silu), r=[pk], w=[sk])

            def ev_u(ps, pk, sg=sg, sk=sk, m=m):
                op("dve", lambda e: e.tensor_tensor(out=self.actb[:, m, :], in0=sg, in1=ps, op=ALU.mult), r=[sk, pk], w=[("actb", m)])
            self.pipe_flush()
            self.pipe_add(self.mm_fm(self.fwin, "fwin", m * P, u, "u"), ev_g)
            self.pipe_add(self.mm_fm(self.fwin, "fwin", FFN + m * P, u, "u"), ev_u)
        self.pipe_flush()

    def ffn_tail(self, L, ti, final):
        op = self.op
        par = ti % 2
        h = self.h[par]
        hk = [("h", par, k) for k in range(8)]
        self.out_proj_residual(self.fwout, "fwout", NFC, self.actb, "actb", h, hk, self.pcol(L, "g2"))
        if final:
            self.rstd(h, hk)
            fn = self.vcol("fn")
            for k in range(8):
                op("dve", lambda e, k=k: e.scalar_tensor_tensor(out=h[:, k, :], in0=h[:, k, :], scalar=fn[:, k:k + 1], in1=self.rs,
                                                                op0=ALU.mult, op1=ALU.mult),
                   r=[hk[k], ("rs",), ("vecs",)], w=[hk[k]])

    def att_phase(self, L, src, dst, first):
        op = self.op
        j = L - 2
        self.reset_alloc()
        self.pbanks = [0, 1]
        self.wq = self.bf(8 * D).rearrange("p (k n) -> p k n", k=8)
        self.wo = self.bf(8 * D).rearrange("p (k n) -> p k n", k=8)
        if L == 2:
            self.kvw = self.bf(8 * 2 * D).rearrange("p (k n) -> p k n", k=8)
            self.ukv = self.bf(8 * T).rearrange("p (k t) -> p k t", k=8)
        self.u = self.bf(8 * T).rearrange("p (k t) -> p k t", k=8)
        self.QT = self.bf(8 * T).rearrange("p (k t) -> p k t", k=8)
        self.on = self.bf(8 * T).rearrange("p (k t) -> p k t", k=8)
        self.NR = 3 if L == 2 else 4
        self.KT = [self.bf(8 * T).rearrange("p (k t) -> p k t", k=8) for _ in range(self.NR)]
        self.VV = [self.bf(NPAIR * 2 * D).rearrange("p (b n) -> p b n", b=NPAIR) for _ in range(self.NR)]
        self.PT = [self.bf(T) for _ in range(4)]
        self.h = [self.f32(8 * T).rearrange("p (k t) -> p k t", k=8) for _ in range(2)]
        self.biasb = self.bf(16 * 640).rearrange("p (h n) -> p h n", h=16)
        self.QTz = [self.bf(8 * T).rearrange("p (k t) -> p k t", k=8) for _ in range(2)]
        self.lnt = self.f32(T)
        self.rs = self.f32(T)
        self.tmp = [self.f32(T) for _ in range(2)]
        self.rden = [self.f32(T) for _ in range(2)]
        self.ptr = 0
        self.kv_from_in = (L == 3 and first)
        self.freeze()
        self.load_w(self.wq, self.b_w_q[j], 8, "wq")
        if L == 2:
            self.load_w(self.kvw, self.kv_w, 8, "kvw")
        self.load_w(self.wo, self.b_w_o[j], 8, "wo")
        op("pool", lambda e: e.dma_start(out=self.biasb.rearrange("p h n -> p (h n)"), in_=self.biasblk[j]), w=[("bias",)], dma=True)
        for hh_ in range(2):
            op("pool", lambda e, hh_=hh_: e.memset(self.QTz[hh_].rearrange("p k t -> p (k t)"), 0.0), w=[("QT", m) for m in range(8)])
        if L == 2:
            for rg_ in range(self.NR):
                v5 = self.VV[rg_].rearrange("p b (j two c) -> p b j two c", two=2, c=P)
                for tb_ in range(NPAIR):
                    op("pool", lambda e, v5=v5, tb_=tb_: e.memset(v5[:, tb_, :, 0, 64:128], 1.0), w=[("VV", rg_, tb_, nq) for nq in range(4)])
                    op("pool", lambda e, v5=v5, tb_=tb_: e.memset(v5[:, tb_, :, 1, 0:64], 1.0), w=[("VV", rg_, tb_, nq) for nq in range(4)])
        srckey = "Hin" if (self.h_in and src is self.H_in) else "H"
        self.kv_from_in = (L == 3 and first)
        self.load_h(src, 0, srckey)
        if L == 3:
            self.load_kv(0)
        kvp = (self.par[:, 256:264], self.par[:, 264:272]) if L == 2 else (None, None)
        self.pre_tile(0, self.pcol(L, "a1"), self.pcol(L, "b1"), *kvp)
        for ti in range(self.NT):
            if ti + 1 < self.NT:
                self.load_h(src, ti + 1, srckey)
                if L == 3:
                    self.load_kv(ti + 1)
            self.att_tile(L, ti)
            if ti + 1 < self.NT and L == 3:
                self.pre_tile(ti + 1, self.pcol(L, "a1"), self.pcol(L, "b1"), *kvp)
            par = ti % 2
            self.out_proj_residual(self.wo, "wo", 8, self.on, "on", self.h[par], [("h", par, k) for k in range(8)], self.pcol(L, "g1"))
            if ti + 1 < self.NT and L == 2:
                self.pre_tile(ti + 1, self.pcol(L, "a1"), self.pcol(L, "b1"), *kvp)
            self.store_h(dst, ti, "H")

    def kt_dram(self, base, ti):
        return base.rearrange("(k p) t -> p k t", p=P)[:, :, ti * T:(ti + 1) * T]

    def vv_dram(self, base, ti):
        return base[ti * T:(ti + 1) * T, :].rearrange("(b p) n -> p b n", p=P)

    def load_kv(self, ti):
        rg = ti % self.NR
        kt_src = self.KT_in if self.kv_from_in else self.KTs
        vv_src = self.VV_in if self.kv_from_in else self.VVs
        self.op("sp", lambda e: e.dma_start(out=self.KT[rg], in_=self.kt_dram(kt_src, ti)),
                r=[("KTs", ti)], w=[("KT", rg, m) for m in range(8)], dma=True)
        self.op("sp", lambda e: e.dma_start(out=self.VV[rg], in_=self.vv_dram(vv_src, ti)),
                r=[("VVs", ti)], w=[("VV", rg, b, nq) for b in range(NPAIR) for nq in range(4)], dma=True)

    def att_tile(self, L, ti):
        op = self.op
        par = ti % 2
        h = self.h[par]
        hk = [("h", par, k) for k in range(8)]
        u = self.u
        rg = ti % self.NR
        if L == 2:
            KT, VV = self.KT[rg], self.VV[rg]
            for m in range(8):
                self.pipe_add(self.mm_fm(self.kvw, "kvw", m * P, self.ukv, "ukv"), self.ev_act(KT[:, m, :], ("KT", rg, m), AF.Copy))
            for tb in range(NPAIR):
                for nq in range(4):
                    v5 = VV.rearrange("p b (j two c) -> p b j two c", two=2, c=P)

                    def ev(ps, pk, v5=v5, tb=tb, nq=nq):
                        ps4 = ps.rearrange("p (j two c) -> p j two c", two=2, c=64)
                        op("dve", lambda e: e.tensor_copy(out=v5[:, tb, 2 * nq:2 * nq + 2, 0, 0:64], in_=ps4[:, :, 0, :]), r=[pk], w=[("VV", rg, tb, nq)])
                        op("dve", lambda e: e.tensor_copy(out=v5[:, tb, 2 * nq:2 * nq + 2, 1, 64:128], in_=ps4[:, :, 1, :]), r=[pk], w=[("VV", rg, tb, nq)])
                    self.pipe_add(self.mm_tm(self.kvw, "kvw", D + nq * 256, 256, self.ukv, "ukv", tb), ev)
            self.pipe_flush()
            op("sp", lambda e, KT=KT: e.dma_start(out=self.kt_dram(self.KTs, ti), in_=KT),
               r=[("KT", rg, m) for m in range(8)], w=[("KTs", ti)], dma=True)
            op("sp", lambda e, VV=VV: e.dma_start(out=self.vv_dram(self.VVs, ti), in_=VV),
               r=[("VV", rg, b, nq) for b in range(NPAIR) for nq in range(4)], w=[("VVs", ti)], dma=True)
        for m in range(8):
            def evq(ps, pk, m=m):
                op("act", lambda e: e.activation(out=self.QTz[0][0:64, m, :], in_=ps[0:64, :], func=AF.Copy, scale=0.125), r=[pk], w=[("QT", m)])
                op("act", lambda e: e.activation(out=self.QTz[1][64:128, m, :], in_=ps[64:128, :], func=AF.Copy, scale=0.125), r=[pk], w=[("QT", m)])
            self.pipe_add(self.mm_fm(self.wq, "wq", m * P, u, "u"), evq)
        self.pipe_flush()
        blocks = []
        for jb in range(6):
            tj = ti - 2 + jb // 2
            if tj < 0:
                continue
            tb = jb % 2
            if jb == 0:
                qlo, qhi = 0, 1
            elif jb == 5:
                qlo, qhi = 2, 3
            else:
                qlo, qhi = 0, 3
            dlo = qlo + 8 - 2 * jb
            blocks.append((jb, tj % self.NR, tb, qlo, qhi, dlo))
        DEPTH = 2
        pending = []
        psX = [self.psF[:, 6, 0:T], self.psF[:, 1, 0:T]]
        xkey = [("ps", 6), ("ps", 1)]

        def finalize(m):
            rden = self.rden[m % 2]
            rk = ("rden", m % 2)
            op("dve", lambda e: e.reciprocal(out=rden[0:64, :], in_=psX[0][64:128, :]), r=[xkey[0]], w=[rk])
            op("dve", lambda e: e.reciprocal(out=rden[64:128, :], in_=psX[1][0:64, :]), r=[xkey[1]], w=[rk])
            op("dve", lambda e: e.tensor_tensor(out=self.on[0:64, m, :], in0=psX[0][0:64, :], in1=rden[0:64, :], op=ALU.mult),
               r=[xkey[0], rk], w=[("on", m)])
            op("dve", lambda e: e.tensor_tensor(out=self.on[64:128, m, :], in0=psX[1][64:128, :], in1=rden[64:128, :], op=ALU.mult),
               r=[xkey[1], rk], w=[("on", m)])

        for m in range(8):
            for bi, (jb, rgj, tb, qlo, qhi, dlo) in enumerate(blocks):
                nq = (qhi - qlo + 1) * 64
                c0, c1 = qlo * 64, (qhi + 1) * 64
                bc = self.ptr
                self.ptr += 1
                for hh in range(2):
                    head = 2 * m + hh
                    r4 = (2 * bc + hh) % 4
                    sbank = 2 + 2 * hh + bc % 2
                    psS = self.psF[:, sbank, 0:nq]
                    sk = ("ps", sbank)
                    KTb = self.KT[rgj]
                    VVb = self.VV[rgj]
                    vkeys = [("VV", rgj, tb, (head * 64) // 256)]
                    op("pe", lambda e, hh=hh, m=m, tb=tb, c0=c0, c1=c1, psS=psS, KTb=KTb: e.matmul(
                        psS, lhsT=KTb[:, m, tb * P:(tb + 1) * P], rhs=self.QTz[hh][:, m, c0:c1], start=True, stop=False),
                       r=[("KT", rgj, m), ("QT", m)], w=[sk])
                    op("pe", lambda e, psS=psS, head=head, dlo=dlo, nq=nq: e.matmul(
                        psS, lhsT=self.ident, rhs=self.biasb[:, head, dlo * 64: dlo * 64 + nq], start=False, stop=True),
                       r=[("bias",), ("cb",)], w=[sk])
                    PT = self.PT[r4][:, 0:nq]
                    op("act", lambda e, psS=psS, PT=PT: e.activation(out=PT, in_=psS, func=AF.Exp), r=[sk], w=[("PT", r4)])
                    fst = (bi == 0)
                    lst = (bi == len(blocks) - 1)

                    def back(hh=hh, head=head, tb=tb, c0=c0, c1=c1, VVb=VVb, PT=PT, fst=fst, lst=lst, vkeys=vkeys, r4=r4, m=m):
                        op("pe", lambda e: e.matmul(
                            psX[hh][:, c0:c1], lhsT=VVb[:, tb, head * P:(head + 1) * P], rhs=PT, start=fst, stop=lst, skip_group_check=True),
                           r=vkeys + [("PT", r4)], w=[xkey[hh]])
                        if lst and hh == 1:
                            finalize(m)
                    pending.append(back)
                    if len(pending) > DEPTH:
                        pending.pop(0)()
        while pending:
            pending.pop(0)()


_SHARED = ("mod_w", "kv_mod_w", "ffn_w_in", "ffn_w_out", "a_w_in", "a_w_out", "kv_w", "b_w_q", "b_w_o")


def make_in_map(inp, b, S=SEQ, biasblk=None):
    m = {k: np.ascontiguousarray(np.asarray(inp[k], np.float32)) for k in _SHARED}
    m["xT"] = np.ascontiguousarray(np.asarray(inp["x"][b][:S], np.float32).T)
    m["vecs"] = _build_vecs(inp, b)
    m["biasblk"] = biasblk if biasblk is not None else _build_bias_blocks(inp["b_rel_bias"])
    return m


def kernel(**inputs):
    inp = {k: np.asarray(v) for k, v in inputs.items()}
    bld = Builder(S=SEQ, phases=ALL_PHASES)
    nc = bld.build()
    bb = _build_bias_blocks(inp["b_rel_bias"])
    shared = {k: np.ascontiguousarray(np.asarray(inp[k], np.float32)) for k in _SHARED}
    in_maps = []
    for b in range(NB):
        m = dict(shared)
        m["xT"] = np.ascontiguousarray(np.asarray(inp["x"][b], np.float32).T)
        m["vecs"] = _build_vecs(inp, b)
        m["biasblk"] = bb
        in_maps.append(m)
    res = run_bass_kernel_spmd(nc, in_maps, core_ids=list(range(NB)))
    out = np.empty((NB, SEQ, D), np.float32)
    for b in range(NB):
        out[b] = res.results[b]["yT"].T
    return out
```
